# Optimizing a Trainium2 kernel written in Bass

```python
import jax, jax.numpy as jnp
from jax import lax
import numpy as np

D_MODEL = 2048
BATCH = 4
SEQ = 4096
DEPTH = 1

MIX_WIDTH = D_MODEL
ATTN_WIDTH = MIX_WIDTH // 2
LRU_WIDTH = MIX_WIDTH - ATTN_WIDTH
HEAD_DIM = 64
N_Q_HEADS = ATTN_WIDTH // HEAD_DIM
N_KV_HEADS = 4
Q_PER_KV = N_Q_HEADS // N_KV_HEADS
WINDOW = 128
BLOCK = 128
ROPE_THETA = 10000.0
LRU_BLOCK = 64
N_LRU_BLOCKS = LRU_WIDTH // LRU_BLOCK
CONV_WIDTH = 4
CONV_PAD_LEFT = 2
CONV_PAD_RIGHT = CONV_WIDTH - 1 - CONV_PAD_LEFT
LRU_C = 8.0
PEER_HEADS = 8
PEER_QDIM = 256
PEER_HALF = PEER_QDIM // 2
N_KEYS = 128
N_EXPERTS = N_KEYS * N_KEYS
PEER_TOPK = 16
PEER_CHUNK = 128
EPS = 1e-6
NEG = -1e30

Q_COLS = N_Q_HEADS * HEAD_DIM
KV_COLS = N_KV_HEADS * HEAD_DIM
IN_COLS = Q_COLS + 2 * KV_COLS + 2 * LRU_WIDTH

kernel_name = "hymba_swa_rglru_peer_encoder"


def rms_norm(x, g):
    xf = x.astype(jnp.float32)
    y = xf * lax.rsqrt(jnp.mean(xf * xf, axis=-1, keepdims=True) + EPS)
    return (y * g.astype(jnp.float32)).astype(x.dtype)


def rope(x, pos):
    half = HEAD_DIM // 2
    inv = ROPE_THETA ** (-jnp.arange(half, dtype=jnp.float32) / half)
    ang = pos.astype(jnp.float32)[:, None] * inv[None, :]
    cos = jnp.cos(ang)[None, :, None, :]
    sin = jnp.sin(ang)[None, :, None, :]
    xf = x.astype(jnp.float32)
    x1, x2 = xf[..., :half], xf[..., half:]
    return jnp.concatenate([x1 * cos - x2 * sin, x2 * cos + x1 * sin], axis=-1).astype(x.dtype)


def windowed_gqa_sink(q, k, v, sink):
    B, S = q.shape[0], q.shape[1]
    nb = S // BLOCK
    qb = q.reshape(B, nb, BLOCK, N_KV_HEADS, Q_PER_KV, HEAD_DIM)

    def band(t):
        tp = jnp.pad(t, ((0, 0), (BLOCK, BLOCK), (0, 0), (0, 0)))
        tp = tp.reshape(B, nb + 2, BLOCK, N_KV_HEADS, HEAD_DIM)
        return jnp.concatenate([tp[:, :-2], tp[:, 1:-1], tp[:, 2:]], axis=2)

    kb, vb = band(k), band(v)
    s = jnp.einsum('bnqgrd,bnkgd->bngrqk', qb, kb,
                   preferred_element_type=jnp.float32) * (HEAD_DIM ** -0.5)
    qpos = jnp.arange(nb)[:, None] * BLOCK + jnp.arange(BLOCK)[None, :]
    kpos = (jnp.arange(nb)[:, None] - 1) * BLOCK + jnp.arange(3 * BLOCK)[None, :]
    rel = kpos[:, None, :] - qpos[:, :, None]
    valid = (jnp.abs(rel) <= WINDOW) & (kpos[:, None, :] >= 0) & (kpos[:, None, :] < S)
    s = jnp.where(valid[None, :, None, None], s, NEG)
    sk = sink.astype(jnp.float32).reshape(1, 1, N_KV_HEADS, Q_PER_KV, 1, 1)
    m = jnp.maximum(jnp.max(s, axis=-1, keepdims=True), sk)
    p = jnp.exp(s - m)
    p = p / (jnp.sum(p, axis=-1, keepdims=True) + jnp.exp(sk - m))
    o = jnp.einsum('bngrqk,bnkgd->bnqgrd', p.astype(v.dtype), vb)
    return o.reshape(B, S, Q_COLS)


def centred_depthwise_conv(x, w, b):
    C = x.shape[-1]
    y = lax.conv_general_dilated(x, w.reshape(CONV_WIDTH, 1, C), window_strides=(1,),
                                 padding=[(CONV_PAD_LEFT, CONV_PAD_RIGHT)],
                                 dimension_numbers=('NWC', 'WIO', 'NWC'),
                                 feature_group_count=C)
    return y + b


def rg_lru(x, w_a, b_a, w_x, b_x, lam, reverse):
    B, S, _ = x.shape
    xb = x.reshape(B, S, N_LRU_BLOCKS, LRU_BLOCK)
    r = jax.nn.sigmoid(jnp.einsum('bshi,hij->bshj', xb, w_a).reshape(B, S, LRU_WIDTH) + b_a)
    i = jax.nn.sigmoid(jnp.einsum('bshi,hij->bshj', xb, w_x).reshape(B, S, LRU_WIDTH) + b_x)
    log_a = (-LRU_C * r.astype(jnp.float32)) * jax.nn.softplus(-lam.astype(jnp.float32))
    a = jnp.exp(log_a)
    u = jnp.sqrt(-jnp.expm1(2.0 * log_a)) * (i * x).astype(jnp.float32)

    def combine(left, right):
        a1, b1 = left
        a2, b2 = right
        return a1 * a2, a2 * b1 + b2

    _, h = lax.associative_scan(combine, (a, u), axis=1, reverse=reverse)
    return h.astype(x.dtype)


def peer(x, w_pq, sub_k1, sub_k2, u_emb, v_emb):
    B, S, D = x.shape
    T = B * S
    xt = x.reshape(T, D)
    q = (xt @ w_pq).reshape(T, PEER_HEADS, 2, PEER_HALF)
    s1 = jnp.einsum('thd,hkd->thk', q[:, :, 0], sub_k1, preferred_element_type=jnp.float32)
    s2 = jnp.einsum('thd,hkd->thk', q[:, :, 1], sub_k2, preferred_element_type=jnp.float32)
    v1, i1 = lax.top_k(s1, PEER_TOPK)
    v2, i2 = lax.top_k(s2, PEER_TOPK)
    cand = (v1[..., :, None] + v2[..., None, :]).reshape(T, PEER_HEADS, PEER_TOPK * PEER_TOPK)
    cidx = (i1[..., :, None] * N_KEYS + i2[..., None, :]).reshape(T, PEER_HEADS, PEER_TOPK * PEER_TOPK)
    sc, sel = lax.top_k(cand, PEER_TOPK)
    eidx = jnp.take_along_axis(cidx, sel, axis=-1)
    gates = jax.nn.softmax(sc, axis=-1)
    nc = T // PEER_CHUNK
    E = PEER_HEADS * PEER_TOPK

    def expert_chunk(args):
        xc, ic, gc = args
        uc = u_emb[ic]
        act = jax.nn.gelu(jnp.einsum('cd,ced->ce', xc, uc))
        vc = v_emb[ic]
        return jnp.einsum('ce,ced->cd', (gc * act).astype(vc.dtype), vc)

    out = lax.map(expert_chunk, (xt.reshape(nc, PEER_CHUNK, D),
                                 eidx.reshape(nc, PEER_CHUNK, E),
                                 gates.reshape(nc, PEER_CHUNK, E)))
    return out.reshape(B, S, D).astype(x.dtype)


def setup_inputs(seed: int = 0) -> dict:
    key = jax.random.key(seed)
    ks = jax.random.split(key, 32)
    f32 = jnp.float32
    L = DEPTH

    def nrm(k, shape, scale):
        return jax.random.normal(k, shape, f32) * scale

    def gain(k, n):
        return 1.0 + 0.02 * jax.random.normal(k, (L, n), f32)

    def lam_init(k):
        a_c = jax.random.uniform(k, (L, LRU_WIDTH), f32, 0.9, 0.999)
        a = a_c ** (1.0 / LRU_C)
        return jnp.log(a) - jnp.log1p(-a)

    gs = LRU_BLOCK ** -0.5
    return {
        "x": jax.random.normal(ks[0], (BATCH, SEQ, D_MODEL), f32),
        "g_mix": gain(ks[1], D_MODEL),
        "w_in": nrm(ks[2], (L, D_MODEL, IN_COLS), D_MODEL ** -0.5),
        "sink": nrm(ks[3], (L, N_Q_HEADS), 0.5),
        "conv_w": nrm(ks[4], (L, CONV_WIDTH, LRU_WIDTH), CONV_WIDTH ** -0.5),
        "conv_b": nrm(ks[5], (L, LRU_WIDTH), 0.01),
        "fwd_wa": nrm(ks[6], (L, N_LRU_BLOCKS, LRU_BLOCK, LRU_BLOCK), gs),
        "fwd_ba": nrm(ks[7], (L, LRU_WIDTH), 0.01),
        "fwd_wx": nrm(ks[8], (L, N_LRU_BLOCKS, LRU_BLOCK, LRU_BLOCK), gs),
        "fwd_bx": nrm(ks[9], (L, LRU_WIDTH), 0.01),
        "fwd_lam": lam_init(ks[10]),
        "bwd_wa": nrm(ks[11], (L, N_LRU_BLOCKS, LRU_BLOCK, LRU_BLOCK), gs),
        "bwd_ba": nrm(ks[12], (L, LRU_WIDTH), 0.01),
        "bwd_wx": nrm(ks[13], (L, N_LRU_BLOCKS, LRU_BLOCK, LRU_BLOCK), gs),
        "bwd_bx": nrm(ks[14], (L, LRU_WIDTH), 0.01),
        "bwd_lam": lam_init(ks[15]),
        "g_attn_out": gain(ks[16], ATTN_WIDTH),
        "g_lru_out": gain(ks[17], LRU_WIDTH),
        "w_out": nrm(ks[18], (L, MIX_WIDTH, D_MODEL), MIX_WIDTH ** -0.5),
        "g_ffn": gain(ks[19], D_MODEL),
        "w_pq": nrm(ks[20], (L, D_MODEL, PEER_HEADS * PEER_QDIM), D_MODEL ** -0.5),
        "sub_k1": nrm(ks[21], (L, PEER_HEADS, N_KEYS, PEER_HALF), PEER_HALF ** -0.5),
        "sub_k2": nrm(ks[22], (L, PEER_HEADS, N_KEYS, PEER_HALF), PEER_HALF ** -0.5),
        "u_emb": nrm(ks[23], (L, N_EXPERTS, D_MODEL), D_MODEL ** -0.5),
        "v_emb": nrm(ks[24], (L, N_EXPERTS, D_MODEL), (PEER_HEADS * PEER_TOPK) ** -0.5),
        "g_final": 1.0 + 0.02 * jax.random.normal(ks[25], (D_MODEL,), f32),
    }


def reference(x, g_mix, w_in, sink, conv_w, conv_b,
              fwd_wa, fwd_ba, fwd_wx, fwd_bx, fwd_lam,
              bwd_wa, bwd_ba, bwd_wx, bwd_bx, bwd_lam,
              g_attn_out, g_lru_out, w_out, g_ffn,
              w_pq, sub_k1, sub_k2, u_emb, v_emb, g_final):
    B, S, D = x.shape
    pos = jnp.arange(S)
    splits = [Q_COLS, Q_COLS + KV_COLS, Q_COLS + 2 * KV_COLS, Q_COLS + 2 * KV_COLS + LRU_WIDTH]
    for l in range(DEPTH):
        h = rms_norm(x, g_mix[l])
        proj = h @ w_in[l]
        q, k, v, x_gate, x_rec = jnp.split(proj, splits, axis=-1)
        q = rope(q.reshape(B, S, N_Q_HEADS, HEAD_DIM), pos)
        k = rope(k.reshape(B, S, N_KV_HEADS, HEAD_DIM), pos)
        v = v.reshape(B, S, N_KV_HEADS, HEAD_DIM)
        y_attn = windowed_gqa_sink(q, k, v, sink[l])

        c = centred_depthwise_conv(x_rec, conv_w[l], conv_b[l])
        h_rec = (rg_lru(c, fwd_wa[l], fwd_ba[l], fwd_wx[l], fwd_bx[l], fwd_lam[l], False)
                 + rg_lru(c, bwd_wa[l], bwd_ba[l], bwd_wx[l], bwd_bx[l], bwd_lam[l], True))
        y_lru = jax.nn.gelu(x_gate) * h_rec

        y = jnp.concatenate([rms_norm(y_attn, g_attn_out[l]), rms_norm(y_lru, g_lru_out[l])], axis=-1)
        x = x + y @ w_out[l]
        x = x + peer(rms_norm(x, g_ffn[l]), w_pq[l], sub_k1[l], sub_k2[l], u_emb[l], v_emb[l])
    return rms_norm(x, g_final)
```

```python
import contextlib
import numpy as np
import concourse.bass as bass
import concourse.mybir as mybir
from concourse.bass_utils import run_bass_kernel_spmd

F32 = mybir.dt.float32
BF16 = mybir.dt.bfloat16
I32 = mybir.dt.int32
U32 = mybir.dt.uint32
ALU = mybir.AluOpType
AF = mybir.ActivationFunctionType
AX = mybir.AxisListType

P = 128
DM = 2048
SEQ = 4096
OWN = 2048
NT_OWN = 16
NT_ALL = 32
INC = 3584
EPS = 1e-6
NEXP = 16384
EPOCH = 8000
DMA_SEM_MAX = 30000
GELU_C = 1.5957691216057308


class Buf:
    __slots__ = ("name", "last_write", "reads", "dsem", "dcount")

    def __init__(self, name):
        self.name = name
        self.last_write = None
        self.reads = {}
        self.dsem = None
        self.dcount = 0


class Prog:
    ENG = ("pe", "act", "dve", "pool", "sp")

    def __init__(self, nc):
        self.nc = nc
        self.q = {e: [] for e in self.ENG}
        self.cnt = {e: 0 for e in self.ENG}
        self.esems = {}
        self.seen = {e: {} for e in self.ENG}
        self.sem_objs = {}
        self.nsem = 0
        self.latest = {}
        self.same_engine_sync = {"pe": False, "act": True, "dve": True, "pool": True, "sp": False}
        self.ninst = 0

    def _new_sem(self, name):
        h = self.nc.alloc_semaphore(name=name)
        self.nsem += 1
        key = self.nsem
        self.sem_objs[key] = h
        return key

    def _esem(self, eng, epoch):
        k = (eng, epoch)
        if k not in self.esems:
            self.esems[k] = self._new_sem(f"e_{eng}_{epoch}")
        return self.esems[k]

    def _deps(self, eng, reads, writes):
        toks = []
        for b in reads:
            if b.last_write is not None:
                toks.append(b.last_write)
        for b in writes:
            if b.last_write is not None:
                toks.append(b.last_write)
            toks.extend(b.reads.values())
        need = {}
        for (semkey, val, teng) in toks:
            if teng == eng and not self.same_engine_sync[eng]:
                continue
            if self.seen[eng].get(semkey, 0) >= val:
                continue
            if need.get(semkey, 0) < val:
                need[semkey] = val
        for semkey, val in need.items():
            self.seen[eng][semkey] = val
        return list(need.items())

    def _record(self, tok, reads, writes):
        for b in reads:
            old = b.reads.get(tok[0])
            if old is None or old[1] < tok[1]:
                b.reads[tok[0]] = tok
        for b in writes:
            b.last_write = tok
            b.reads = {}
        self.latest[tok[0]] = tok

    def op(self, eng, fn, reads=(), writes=()):
        waits = self._deps(eng, reads, writes)
        self.cnt[eng] += 1
        epoch, val = divmod(self.cnt[eng] - 1, EPOCH)
        semkey = self._esem(eng, epoch)
        tok = (semkey, val + 1, eng)
        self.q[eng].append((fn, waits, (semkey, 1)))
        self._record(tok, reads, writes)
        self.ninst += 1

    def dma(self, eng, fn, reads=(), writes=()):
        waits = self._deps(eng, reads, writes)
        b = writes[0]
        if b.dsem is None or b.dcount + 16 > DMA_SEM_MAX:
            b.dsem = self._new_sem(f"d_{b.name}")
            b.dcount = 0
        b.dcount += 16
        tok = (b.dsem, b.dcount, "dma")
        self.q[eng].append((fn, waits, (b.dsem, 16)))
        self._record(tok, reads, writes)
        self.ninst += 1

    def barrier(self):
        for eng in self.ENG:
            need = []
            for semkey, (sk, val, teng) in self.latest.items():
                if self.seen[eng].get(semkey, 0) >= val:
                    continue
                if teng == eng and eng in ("pe", "sp"):
                    pass
                self.seen[eng][semkey] = val
                need.append((semkey, val))
            if need:
                self.q[eng].append((None, need, None))

    def emit(self):
        nc = self.nc
        objs = self.sem_objs
        qs = self.q
        self.q = {e: [] for e in self.ENG}

        def run(engname):
            def body(e):
                for fn, waits, inc in qs[engname]:
                    for semkey, val in waits:
                        e.wait_ge(objs[semkey], val)
                    if fn is not None:
                        fn(e).then_inc(objs[inc[0]], inc[1])
            return body

        with nc.Block() as block:
            block.tensor(run("pe"))
            block.scalar(run("act"))
            block.vector(run("dve"))
            block.gpsimd(run("pool"))
            block.sync(run("sp"))


class T:
    def __init__(self, t, name, psum=False):
        self.t = t
        self.b = Buf(name)
        self.psum = psum

    def __getitem__(self, k):
        return self.t[k]


def build_program(debug=False, stop_after=None, only=None):
    nc = bass.Bass("TRN2", target_bir_lowering=False)
    pg = Prog(nc)

    def din(name, shape, dt=F32):
        return nc.dram_tensor(name, list(shape), dt, kind="ExternalInput").ap()

    def dscr(name, shape, dt=F32):
        kind = "ExternalOutput" if debug else "Internal"
        return T(nc.dram_tensor(name, list(shape), dt, kind=kind).ap(), name)

    x_d = din("x", [SEQ, DM])
    w_in_d = din("w_in", [DM, INC])
    w_out_d = din("w_out", [DM, DM])
    w_pq_d = din("w_pq", [DM, DM])
    u_d = din("u_emb", [NEXP, DM])
    v_d = din("v_emb", [NEXP, DM])
    cs_d = din("rope_cs", [17 * P, 64])
    sn_d = din("rope_sn", [17 * P, 64])
    ident_d = din("ident", [P, P])
    mprev_d = din("mask_prev", [P, P])
    mnext_d = din("mask_next", [P, P])
    gm_d = din("gm", [P, 16])
    gao_d = din("gao", [P, 8])
    glo_d = din("glo", [P, 8])
    sink_d = din("sinkb", [P, 16])
    w5_d = din("w5", [P, 8 * 5])
    cvb_d = din("cvb", [P, 8])
    gw_d = din("gatew", [4, 8, P, P])
    gb_d = din("gateb", [P, 4 * 8])
    lam_d = din("lam", [P, 2 * 8])
    gbf_d = din("gffn_b", [P, DM])
    gbl_d = din("gfinal_b", [P, DM])
    subk_d = din("subkT", [P, 16, P])
    iota_d = din("iota256", [P, 256])
    i16_d = din("iota16", [P, 32])
    out_d = T(nc.dram_tensor("out", [OWN, DM], F32, kind="ExternalOutput").ap(), "out")

    xrT_d = dscr("xrT", [8, P, SEQ])
    xgT_d = dscr("xgT", [8, P, OWN])
    yaT_d = dscr("yaT", [NT_OWN, P, 1024], BF16)
    rsa_d = dscr("rsa", [NT_OWN, P, 1])
    ylT_d = dscr("ylT", [8, P, OWN], BF16)
    ysq_d = dscr("ysq", [8, P, OWN], BF16)
    x1_d = dscr("x1", [OWN, DM])
    idx_d = dscr("eidx", [NT_OWN, P, P], I32)
    gate_d = dscr("gate", [NT_OWN, P, P])
    uvb_d = dscr("uvb", [NEXP, 2 * DM], BF16)
    xnb_d = dscr("xnbs", [OWN, DM], BF16)
    sc_d = dscr("scs", [NT_OWN, P, 16 * P])

    def OP(eng, method, reads, writes, **kw):
        rd = [r.b for r in reads if not r.psum]
        wr = [w.b for w in writes] + [r.b for r in reads if r.psum]
        pg.op(eng, lambda e: getattr(e, method)(**kw), rd, wr)

    def V(method, reads, writes, **kw):
        OP("dve", method, reads, writes, **kw)

    def A(method, reads, writes, **kw):
        OP("act", method, reads, writes, **kw)

    def G(method, reads, writes, **kw):
        OP("pool", method, reads, writes, **kw)

    def M(reads, writes, **kw):
        OP("pe", "matmul", reads, writes, **kw)

    def TR(reads, writes, **kw):
        OP("pe", "transpose", reads, writes, **kw)

    def D(eng, out, in_, reads, writes):
        pg.dma(eng, lambda e: e.dma_start(out=out, in_=in_), [r.b for r in reads], [w.b for w in writes])

    def GATHER(out, tab, off, reads, writes):
        pg.dma("pool", lambda e: e.indirect_dma_start(
            out=out, out_offset=None, in_=tab, in_offset=bass.IndirectOffsetOnAxis(ap=off, axis=0)),
            [r.b for r in reads], [w.b for w in writes])

    def bc(ap, shape, axis):
        return ap.unsqueeze(axis).to_broadcast(list(shape))

    gstack = contextlib.ExitStack()

    def mk(stack):
        def S(name, shape, dt=F32):
            return T(stack.enter_context(nc.sbuf_tensor("s_" + name, list(shape), dt)), name)

        def PS(name, shape, dt=F32):
            return T(stack.enter_context(nc.psum_tensor("p_" + name, list(shape), dt)), name, psum=True)
        return S, PS

    with gstack:
        GS, _ = mk(gstack)
        identf = GS("identf", [P, P])
        identb = GS("identb", [P, P], BF16)
        onesb = GS("onesb", [P, 1], BF16)
        onesf = GS("onesf", [P, 1])
        epsb = GS("epsb", [P, 1])
        D("sp", identf[:], ident_d, [], [identf])
        V("tensor_copy", [identf], [identb], out=identb[:], in_=identf[:])
        V("memset", [], [onesb], ap=onesb[:], constant=1.0)
        V("memset", [], [onesf], ap=onesf[:], constant=1.0)
        V("memset", [], [epsb], ap=epsb[:], constant=EPS)

        def rstd_from_ss(ss, out, n):
            A("activation", [ss, epsb], [out], out=out[:], in_=ss[:, 0:1], func=AF.Sqrt, scale=1.0 / n, bias=epsb[:])
            V("reciprocal", [out], [out], out=out[:], in_=out[:])

        with contextlib.ExitStack() as st:
            S, PS = mk(st)
            Win = [S(f"win{kc}", [P, INC], BF16) for kc in range(16)]
            gm = S("gm", [P, 16]); gao = S("gao", [P, 8]); es = S("es", [P, 16])
            mprev = S("mprev", [P, P], BF16); mnext = S("mnext", [P, P], BF16)
            mtmp = S("mtmp", [P, P])
            D("sp", gm[:], gm_d, [], [gm])
            D("sp", gao[:], gao_d, [], [gao])
            D("sp", es[:], sink_d, [], [es])
            A("activation", [es], [es], out=es[:], in_=es[:], func=AF.Exp)
            D("sp", mtmp[:], mprev_d, [], [mtmp])
            V("tensor_copy", [mtmp], [mprev], out=mprev[:], in_=mtmp[:])
            D("sp", mtmp[:], mnext_d, [], [mtmp])
            V("tensor_copy", [mtmp], [mnext], out=mnext[:], in_=mtmp[:])
            for kc in range(16):
                D("pool", Win[kc][:], w_in_d[kc * P:(kc + 1) * P, :], [], [Win[kc]])

            xt = [S(f"xt{i}", [P, DM]) for i in range(2)]
            junkb = S("junkb", [P, DM], BF16)
            ss = S("ss", [P, 1]); rstd = S("rstd", [P, 1])
            hbs = [S(f"hb{i}", [P, DM], BF16) for i in range(2)]
            hTs = [S(f"hT{i}", [P, 16, P], BF16) for i in range(2)]
            css = [S(f"cs{i}", [P, 64]) for i in range(2)]; sns = [S(f"sn{i}", [P, 64]) for i in range(2)]
            cs = css[0]; sn = sns[0]
            t1 = S("t1", [P, 512]); t2 = S("t2", [P, 512])
            qb = S("qb", [P, 1024], BF16); kb = S("kb", [P, 256], BF16)
            qT = [S(f"qT{i}", [64, 16 * P], BF16) for i in range(2)]
            kT = [S(f"kT{i}", [64, 4 * P], BF16) for i in range(4)]
            va = [S(f"va{i}", [P, 4, 65], BF16) for i in range(4)]
            Pb = [S(f"Pb{i}", [P, 512], BF16) for i in range(3)]
            den = S("den", [P, 4])
            yat = S("yat", [P, 16, 64]); yab = S("yab", [P, 1024], BF16)
            yaT = S("yaT", [P, 8, P], BF16)
            ssa = S("ssa", [P, 1]); rsa = S("rsa", [P, 1])
            fm = [S(f"fm{i}", [P, 4, P]) for i in range(2)]
            pj = [PS(f"pj{i}", [P, 512]) for i in range(2)]
            pT = PS("pT", [P, DM], BF16)
            scp = [PS(f"scp{i}", [P, 512]) for i in range(3)]
            pv = PS("pv", [P, 512])
            for i in range(4):
                V("memset", [], [va[i]], ap=va[i][:], constant=1.0)

            def rope(src, nh, dst, srcT, dstT):
                w = nh * 64
                s3 = src.rearrange("p (h d) -> p h d", h=nh)
                s4 = src.rearrange("p (h t d) -> p h t d", h=nh, t=2)[:, :, ::-1, :]
                V("tensor_tensor", [srcT, cs], [t1], out=t1[:, 0:w].rearrange("p (h d) -> p h d", h=nh),
                  in0=s3, in1=bc(cs[:], [P, nh, 64], 1), op=ALU.mult)
                V("tensor_tensor", [srcT, sn], [t2], out=t2[:, 0:w].rearrange("p (h t d) -> p h t d", h=nh, t=2),
                  in0=s4, in1=bc(sn[:].rearrange("p (t d) -> p t d", t=2), [P, nh, 2, 32], 1), op=ALU.mult)
                G("tensor_tensor", [t1, t2], [dstT], out=dst, in0=t1[:, 0:w], in1=t2[:, 0:w], op=ALU.add)

            def attention(t, fm_thunks=()):
                fm_thunks = list(fm_thunks)
                blocks = [b for b in (t - 1, t, t + 1) if b >= 0]
                qTt = qT[t % 2]
                for g in range(4):
                    for bi, b in enumerate(blocks):
                        M([kT[b % 4], qTt], [scp[bi]], out=scp[bi][:], lhsT=kT[b % 4][:, g * P:(g + 1) * P],
                          rhs=qTt[:, 4 * g * P:(4 * g + 4) * P], start=True, stop=True)
                        A("activation", [scp[bi]], [Pb[bi]], out=Pb[bi][:], in_=scp[bi][:], func=AF.Exp, scale=0.125)
                        if b != t:
                            mk_ = mprev if b == t - 1 else mnext
                            v3 = Pb[bi][:].rearrange("p (r q) -> p r q", r=4)
                            G("tensor_tensor", [Pb[bi], mk_], [Pb[bi]], out=v3, in0=v3,
                              in1=bc(mk_[:], [P, 4, P], 1), op=ALU.mult)
                    if fm_thunks:
                        fm_thunks.pop(0)()
                    for r in range(4):
                        for bi, b in enumerate(blocks):
                            M([Pb[bi], va[b % 4]], [pv], out=pv[:, r * 65:(r + 1) * 65], lhsT=Pb[bi][:, r * P:(r + 1) * P],
                              rhs=va[b % 4][:, g, :], start=(bi == 0), stop=(bi == len(blocks) - 1))
                    pv3 = pv[:, 0:260].rearrange("p (r d) -> p r d", r=4)
                    V("tensor_tensor", [pv, es], [den], out=den[:], in0=pv3[:, :, 64], in1=es[:, 4 * g:4 * g + 4], op=ALU.add)
                    V("reciprocal", [den], [den], out=den[:], in_=den[:])
                    V("tensor_tensor", [pv, den], [yat], out=yat[:, 4 * g:4 * g + 4, :], in0=pv3[:, :, 0:64],
                      in1=bc(den[:], [P, 4, 64], 2), op=ALU.mult)
                while fm_thunks:
                    fm_thunks.pop(0)()
                yflat = yat[:].rearrange("p h d -> p (h d)")
                A("activation", [yat], [junkb, ssa], out=junkb[:, 0:1024], in_=yflat, func=AF.Square, accum_out=ssa[:])
                rstd_from_ss(ssa, rsa, 1024)
                D("sp", rsa_d[t], rsa[:], [rsa], [rsa_d])
                A("activation", [yat], [yab], out=yab[:], in_=yflat, func=AF.Copy)
                for kc in range(8):
                    TR([yab, identb], [pT], out=pT[:, kc * P:(kc + 1) * P], in_=yab[:, kc * P:(kc + 1) * P], identity=identb[:])
                V("tensor_tensor", [pT, gao], [yaT], out=yaT[:], in0=pT[:, 0:1024].rearrange("p (k t) -> p k t", k=8),
                  in1=bc(gao[:], [P, 8, P], 2), op=ALU.mult)
                D("sp", yaT_d[t], yaT[:].rearrange("p k t -> p (k t)"), [yaT], [yaT_d])

            _AT = NT_ALL
            def load_x(i):
                D("sp", xt[i % 2][:], x_d[i * P:(i + 1) * P, :], [], [xt[i % 2]])
                if i <= NT_OWN:
                    D("sp", css[i % 2][:], cs_d[i * P:(i + 1) * P, :], [], [css[i % 2]])
                    D("sp", sns[i % 2][:], sn_d[i * P:(i + 1) * P, :], [], [sns[i % 2]])

            def front(i):
                A("activation", [xt[i % 2]], [junkb, ss], out=junkb[:], in_=xt[i % 2][:], func=AF.Square, accum_out=ss[:])
                rstd_from_ss(ss, rstd, DM)
                A("activation", [xt[i % 2], rstd], [hbs[i % 2]], out=hbs[i % 2][:], in_=xt[i % 2][:], func=AF.Copy, scale=rstd[:])

            for i in range(_AT):
                own = i < NT_OWN
                hb = hbs[i % 2]; hT = hTs[i % 2]
                _tab, _c0 = (u_d, 0) if i < 16 else (v_d, DM)
                _r0 = (i % 16) * 1024
                D("pool", uvb_d[_r0:_r0 + 1024, _c0:_c0 + DM], _tab[_r0:_r0 + 1024, :], [], [uvb_d])
                x_t = xt[i % 2]
                cs = css[i % 2]; sn = sns[i % 2]
                if i == 0:
                    load_x(0)
                if i + 1 < _AT:
                    load_x(i + 1)
                if i == 0:
                    front(0)
                for kc in range(16):
                    TR([hb, identb], [pT], out=pT[:, kc * P:(kc + 1) * P], in_=hb[:, kc * P:(kc + 1) * P], identity=identb[:])
                V("tensor_tensor", [pT, gm], [hT], out=hT[:], in0=pT[:].rearrange("p (k t) -> p k t", k=16),
                  in1=bc(gm[:], [P, 16, P], 2), op=ALU.mult)
                if i <= NT_OWN:
                    pk = pj[0]
                    for kc in range(16):
                        M([hT, Win[kc]], [pk], out=pk[:], lhsT=hT[:, kc, :], rhs=Win[kc][:, 1024:1536],
                          start=(kc == 0), stop=(kc == 15))
                    rope(pk[:, 0:256], 4, kb[:], pk, kb)
                    A("activation", [pk], [va[i % 4]], out=va[i % 4][:, :, 0:64],
                      in_=pk[:, 256:512].rearrange("p (g d) -> p g d", g=4), func=AF.Copy)
                if own:
                    for grp in range(2):
                        pq = pj[1 - grp]
                        for kc in range(16):
                            M([hT, Win[kc]], [pq], out=pq[:], lhsT=hT[:, kc, :], rhs=Win[kc][:, grp * 512:(grp + 1) * 512],
                              start=(kc == 0), stop=(kc == 15))
                        rope(pq[:], 8, qb[:, grp * 512:(grp + 1) * 512], pq, qb)
                if i <= NT_OWN:
                    for g in range(4):
                        TR([kb, identb], [pT], out=pT[0:64, g * P:(g + 1) * P], in_=kb[:, g * 64:(g + 1) * 64], identity=identb[:])
                    V("tensor_copy", [pT], [kT[i % 4]], out=kT[i % 4][:], in_=pT[0:64, 0:512])
                if i + 1 < _AT:
                    front(i + 1)
                jobs = []
                if own:
                    jobs.append((1536, xgT_d))
                jobs.append((2560, xrT_d))
                fm_thunks = []
                nb = 0
                for col0, dst in jobs:
                    for c4 in range(2):
                        def fm_chunk(col0=col0, dst=dst, c4=c4, nb=nb, hT=hT, i=i):
                            pq = pj[nb % 2]; f = fm[nb % 2]
                            for cc in range(4):
                                c = c4 * 4 + cc
                                for kc in range(16):
                                    M([hT, Win[kc]], [pq], out=pq[:, cc * P:(cc + 1) * P],
                                      lhsT=Win[kc][:, col0 + c * P:col0 + (c + 1) * P], rhs=hT[:, kc, :],
                                      start=(kc == 0), stop=(kc == 15))
                            A("activation", [pq], [f], out=f[:].rearrange("p c t -> p (c t)"), in_=pq[:], func=AF.Copy)
                            D("sp", dst[c4 * 4:(c4 + 1) * 4, :, i * P:(i + 1) * P].rearrange("c p t -> p c t"), f[:], [f], [dst])
                        fm_thunks.append(fm_chunk)
                        nb += 1
                if 1 <= i <= NT_OWN:
                    attention(i - 1, fm_thunks)
                else:
                    for f_ in fm_thunks:
                        f_()
                if own:
                    qTt = qT[i % 2]
                    for h in range(16):
                        TR([qb, identb], [pT], out=pT[0:64, h * P:(h + 1) * P], in_=qb[:, h * 64:(h + 1) * 64], identity=identb[:])
                    V("tensor_copy", [pT], [qTt], out=qTt[:], in_=pT[0:64, :])
            pg.barrier()
            pg.emit()
            if stop_after == "A":
                return nc, pg

        with contextlib.ExitStack() as st:
            S, PS = mk(st)
            w5 = S("w5", [P, 40]); cvb = S("cvb", [P, 8]); gb = S("gb", [P, 32]); lam = S("lam", [P, 16])
            kap = S("kap", [P, 16]); kap2 = S("kap2", [P, 16]); glo = S("glo", [P, 8])
            gw = [[S(f"gw{k}_{c}", [P, P], BF16) for c in range(8)] for k in range(4)]
            for a_, d_ in ((w5, w5_d), (cvb, cvb_d), (gb, gb_d), (lam, lam_d), (glo, glo_d)):
                D("sp", a_[:], d_, [], [a_])
            for k in range(4):
                for c in range(8):
                    gw[k][c].b = gw[0][0].b
                    D("pool", gw[k][c][:], gw_d[k, c], [], [gw[k][c]])
            A("activation", [lam], [kap], out=kap[:], in_=lam[:], func=AF.Exp, scale=-1.0)
            A("activation", [kap, onesf], [kap], out=kap[:], in_=kap[:], func=AF.Ln, bias=onesf[:], scale=1.0)
            V("tensor_scalar", [kap], [kap2], out=kap2[:], in0=kap[:], scalar1=-16.0, scalar2=None, op0=ALU.mult)
            V("tensor_scalar", [kap], [kap], out=kap[:], in0=kap[:], scalar1=-8.0, scalar2=None, op0=ALU.mult)

            xrs = [S(f"xr{i}", [P, SEQ + 4]) for i in range(2)]; ccs = [S(f"cc{i}", [P, SEQ]) for i in range(2)]
            cbs = [S(f"cb{i}", [P, SEQ], BF16) for i in range(2)]
            Rb = [S(f"Rb{i}", [P, OWN]) for i in range(2)]; Ib = [S(f"Ib{i}", [P, OWN]) for i in range(2)]
            Qb = [S(f"Qb{i}", [P, OWN]) for i in range(2)]
            st_ = S("lru_state", [P, 1])
            h2 = S("h2", [P, OWN]); h1 = S("h1", [P, OWN])
            xg = S("xg", [P, OWN]); tg = S("tg", [P, OWN])
            ylb = S("ylb", [P, OWN], BF16); ysb = S("ysb", [P, OWN], BF16)
            pgt = [PS(f"pgt{i}", [P, 512]) for i in range(4)]
            for xr in xrs:
                V("memset", [], [xr], ap=xr[:, 0:2], constant=0.0)
                V("memset", [], [xr], ap=xr[:, SEQ + 2:SEQ + 4], constant=0.0)
            npg = 0
            njob = [0]

            def conv(c):
                xr = xrs[c % 2]; cc_ = ccs[c % 2]; cb = cbs[c % 2]
                V("tensor_scalar", [xr, w5, cvb], [cc_], out=cc_[:], in0=xr[:, 0:SEQ], scalar1=w5[:, c * 5:c * 5 + 1],
                  scalar2=cvb[:, c:c + 1], op0=ALU.mult, op1=ALU.add)
                for o in range(1, 5):
                    V("scalar_tensor_tensor", [xr, w5, cc_], [cc_], out=cc_[:], in0=xr[:, o:o + SEQ],
                      scalar=w5[:, c * 5 + o:c * 5 + o + 1], in1=cc_[:], op0=ALU.mult, op1=ALU.add)

            D("sp", xrs[0][:, 2:SEQ + 2], xrT_d[0], [xrT_d], [xrs[0]])
            D("sp", xg[:], xgT_d[0], [xgT_d], [xg])
            conv(0)
            for c in range(8):
                cc_ = ccs[c % 2]; cb = cbs[c % 2]
                if c + 1 < 8:
                    D("sp", xrs[(c + 1) % 2][:, 2:SEQ + 2], xrT_d[c + 1], [xrT_d], [xrs[(c + 1) % 2]])
                A("activation", [cc_], [cb], out=cb[:], in_=cc_[:], func=AF.Copy)
                for job, (dr, tok0, rev) in enumerate(((1, OWN, True), (1, 0, True), (0, 0, False))):
                    n_ = njob[0]; njob[0] += 1
                    R_ = Rb[n_ % 2]; I_ = Ib[n_ % 2]; Q_ = Qb[n_ % 2]
                    for tgp in range(OWN // 512):
                        sl = slice(tok0 + tgp * 512, tok0 + (tgp + 1) * 512)
                        dsl = slice(tgp * 512, (tgp + 1) * 512)
                        for (kk, dstt) in ((0, R_), (1, I_)):
                            k = dr * 2 + kk
                            pq = pgt[npg % 4]; npg += 1
                            M([gw[k][c], cb], [pq], out=pq[:], lhsT=gw[k][c][:], rhs=cb[:, sl], start=True, stop=True)
                            A("activation", [pq, gb], [dstt], out=dstt[:, dsl], in_=pq[:], func=AF.Sigmoid,
                              bias=gb[:, k * 8 + c:k * 8 + c + 1], scale=1.0)
                    kc_ = dr * 8 + c
                    A("activation", [R_, kap2], [Q_], out=Q_[:], in_=R_[:], func=AF.Exp, scale=kap2[:, kc_:kc_ + 1])
                    A("activation", [R_, kap], [R_], out=R_[:], in_=R_[:], func=AF.Exp, scale=kap[:, kc_:kc_ + 1])
                    A("activation", [Q_, onesf], [Q_], out=Q_[:], in_=Q_[:], func=AF.Sqrt, scale=-1.0, bias=onesf[:])
                    V("tensor_tensor", [Q_, I_], [I_], out=I_[:], in0=I_[:], in1=Q_[:], op=ALU.mult)
                    V("tensor_tensor", [cc_, I_], [I_], out=I_[:], in0=I_[:], in1=cc_[:, tok0:tok0 + OWN], op=ALU.mult)
                    if job == 0:
                        V("tensor_tensor_scan", [R_, I_], [Q_], out=Q_[:, ::-1], data0=R_[:, ::-1], data1=I_[:, ::-1],
                          initial=0.0, op0=ALU.mult, op1=ALU.add)
                        V("tensor_copy", [Q_], [st_], out=st_[:], in_=Q_[:, 0:1])
                        if c + 1 < 8:
                            conv(c + 1)
                    elif job == 1:
                        V("tensor_tensor_scan", [R_, I_, st_], [h2], out=h2[:, ::-1], data0=R_[:, ::-1], data1=I_[:, ::-1],
                          initial=st_[:, 0:1], op0=ALU.mult, op1=ALU.add)
                    else:
                        V("tensor_tensor_scan", [R_, I_], [h1], out=h1[:], data0=R_[:], data1=I_[:], initial=0.0,
                          op0=ALU.mult, op1=ALU.add)
                G("tensor_tensor", [h1, h2], [h1], out=h1[:], in0=h1[:], in1=h2[:], op=ALU.add)
                A("activation", [xg], [tg], out=tg[:], in_=xg[:], func=AF.Gelu_apprx_tanh)
                if c + 1 < 8:
                    D("sp", xg[:], xgT_d[c + 1], [xgT_d], [xg])
                V("tensor_tensor", [tg, h1], [tg], out=tg[:], in0=tg[:], in1=h1[:], op=ALU.mult)
                G("tensor_tensor", [tg], [ysb], out=ysb[:], in0=tg[:], in1=tg[:], op=ALU.mult)
                V("tensor_scalar", [tg, glo], [ylb], out=ylb[:], in0=tg[:], scalar1=glo[:, c:c + 1], scalar2=None, op0=ALU.mult)
                D("sp", ysq_d[c], ysb[:], [ysb], [ysq_d])
                D("sp", ylT_d[c], ylb[:], [ylb], [ylT_d])
            pg.barrier()
            pg.emit()
            if stop_after == "B":
                return nc, pg

        with contextlib.ExitStack() as st:
            S, PS = mk(st)
            Wo = [S(f"wo{kc}", [P, DM], BF16) for kc in range(16)]
            Wq = [S(f"wq{kc}", [P, DM], BF16) for kc in range(16)]
            for kc in range(16):
                Wo[kc].b = Wo[0].b
                D("pool", Wo[kc][:], w_out_d[kc * P:(kc + 1) * P, :], [], [Wo[kc]])
            for kc in range(16):
                Wq[kc].b = Wq[0].b
                D("pool", Wq[kc][:], w_pq_d[kc * P:(kc + 1) * P, :], [], [Wq[kc]])
            subk = S("subk", [P, 16, P]); gbf = S("gbf", [P, DM])
            D("sp", subk[:], subk_d, [], [subk])
            D("sp", gbf[:], gbf_d, [], [gbf])
            ya = S("ya", [P, 1024], BF16); yl = S("yl", [P, 8, P], BF16); ysq = S("ysqt", [P, 8, P], BF16)
            rsa = S("rsa_c", [P, 1]); rsl = S("rsl", [P, 1])
            xtt = S("xt_c", [P, DM]); x1 = S("x1_c", [P, DM])
            junkb = S("junkb_c", [P, DM], BF16)
            ss = S("ss_c", [P, 1]); rsf = S("rsf_c", [P, 1])
            xnb = S("xnb_c", [P, DM], BF16); xnT = S("xnT_c", [P, 16, P], BF16)
            qTf = S("qTf", [P, 16, P]); ssb = S("ssb_c", [P, 16, P])
            pa = PS("pa", [P, 512]); pl = PS("pl", [P, 512]); pss = PS("pss", [P, 512])
            pT = PS("pT_c", [P, DM], BF16)
            pq2 = [PS(f"pq2_{i}", [P, 512]) for i in range(2)]
            psc = PS("psc", [P, 512])
            def loads_c(t):
                tsl = slice(t * P, (t + 1) * P)
                D("sp", ya[:], yaT_d[t], [yaT_d], [ya])
                D("sp", yl[:], ylT_d[:, :, tsl].rearrange("c p t -> p c t"), [ylT_d], [yl])
                D("sp", ysq[:], ysq_d[:, :, tsl].rearrange("c p t -> p c t"), [ysq_d], [ysq])
                D("sp", rsa[:], rsa_d[t], [rsa_d], [rsa])
                D("sp", xtt[:], x_d[tsl, :], [], [xtt])

            loads_c(0)
            for t in range(NT_OWN):
                tsl = slice(t * P, (t + 1) * P)
                for c in range(8):
                    M([ysq, onesb], [pss], out=pss[:, 0:1], lhsT=ysq[:, c, :], rhs=onesb[:, 0:1], start=(c == 0), stop=(c == 7))
                rstd_from_ss(pss, rsl, 1024)
                for n in range(4):
                    nsl = slice(n * 512, (n + 1) * 512)
                    pa_ = (pa, pq2[0])[n % 2]; pl_ = (pl, pq2[1])[n % 2]
                    for kc in range(8):
                        M([ya, Wo[kc]], [pa_], out=pa_[:], lhsT=ya[:, kc * P:(kc + 1) * P], rhs=Wo[kc][:, nsl],
                          start=(kc == 0), stop=(kc == 7))
                    for kc in range(8):
                        M([yl, Wo[8 + kc]], [pl_], out=pl_[:], lhsT=yl[:, kc, :], rhs=Wo[8 + kc][:, nsl],
                          start=(kc == 0), stop=(kc == 7))
                    V("scalar_tensor_tensor", [pa_, rsa, xtt], [x1], out=x1[:, nsl], in0=pa_[:], scalar=rsa[:, 0:1],
                      in1=xtt[:, nsl], op0=ALU.mult, op1=ALU.add)
                    V("scalar_tensor_tensor", [pl_, rsl, x1], [x1], out=x1[:, nsl], in0=pl_[:], scalar=rsl[:, 0:1],
                      in1=x1[:, nsl], op0=ALU.mult, op1=ALU.add)
                if t + 1 < NT_OWN:
                    loads_c(t + 1)
                D("sp", x1_d[tsl, :], x1[:], [x1], [x1_d])
                A("activation", [x1], [junkb, ss], out=junkb[:], in_=x1[:], func=AF.Square, accum_out=ss[:])
                rstd_from_ss(ss, rsf, DM)
                V("scalar_tensor_tensor", [x1, rsf, gbf], [xnb], out=xnb[:], in0=x1[:], scalar=rsf[:, 0:1], in1=gbf[:],
                  op0=ALU.mult, op1=ALU.mult)
                D("sp", xnb_d[tsl, :], xnb[:], [xnb], [xnb_d])
                for kc in range(16):
                    TR([xnb, identb], [pT], out=pT[:, kc * P:(kc + 1) * P], in_=xnb[:, kc * P:(kc + 1) * P], identity=identb[:])
                A("activation", [pT], [xnT], out=xnT[:].rearrange("p k t -> p (k t)"), in_=pT[:], func=AF.Copy)
                for c4 in range(4):
                    pq = pq2[c4 % 2]
                    for cc in range(4):
                        c = c4 * 4 + cc
                        for kc in range(16):
                            M([Wq[kc], xnT], [pq], out=pq[:, cc * P:(cc + 1) * P], lhsT=Wq[kc][:, c * P:(c + 1) * P],
                              rhs=xnT[:, kc, :], start=(kc == 0), stop=(kc == 15))
                    A("activation", [pq], [qTf], out=qTf[:, c4 * 4:(c4 + 1) * 4, :].rearrange("p c t -> p (c t)"),
                      in_=pq[:], func=AF.Copy)
                for s4 in range(4):
                    for s_ in range(4):
                        s = s4 * 4 + s_
                        M([qTf, subk], [psc], out=psc[:, s_ * P:(s_ + 1) * P], lhsT=qTf[:, s, :], rhs=subk[:, s, :],
                          start=True, stop=True)
                    V("tensor_copy", [psc], [ssb], out=ssb[:, s4 * 4:(s4 + 1) * 4, :].rearrange("p c t -> p (c t)"), in_=psc[:])
                D("sp", sc_d[t], ssb[:].rearrange("p c t -> p (c t)"), [ssb], [sc_d])
            pg.barrier()
            pg.emit()
            if stop_after == "C":
                return nc, pg

        with contextlib.ExitStack() as st:
            S, PS = mk(st)
            NR = 16
            GS = 4
            NGT = P // GS
            NG = NT_OWN * NGT
            gbl = S("gbl", [P, DM]); i16 = S("i16", [P, 32])
            D("sp", gbl[:], gbl_d, [], [gbl])
            D("sp", i16[:], i16_d, [], [i16])
            x1t = [S(f"x1t2_{i}", [P, DM]) for i in range(2)]
            xnb = [S(f"xnb2_{i}", [P, DM], BF16) for i in range(2)]
            idx = [S(f"idx2_{i}", [P, P], I32) for i in range(2)]
            gate = [S(f"gate2_{i}", [P, P]) for i in range(2)]
            a4 = [S(f"a4_{i}", [P, GS]) for i in range(3)]
            tq = [S(f"tq4_{i}", [P, GS]) for i in range(2)]
            cf = [S(f"cf4_{i}", [P, GS]) for i in range(3)]
            junka = S("junka", [P, DM], BF16)
            ss3 = S("ss3", [P, 1]); rsf3 = S("rsf3", [P, 1])
            uv = [S(f"uv{i}", [P, 2 * DM], BF16) for i in range(NR)]
            dg = [S(f"dg{i}", [P, P], BF16) for i in range(4)]
            accp = [PS(f"accp{i}", [P, 512]) for i in range(4)]
            ssb = S("ssb_r", [P, 16, P]); E = S("E_r", [P, 2048]); wk = [S(f"wk{i}", [P, P]) for i in range(2)]
            mx = S("mx", [P, 16, 16]); mxi = S("mxi", [P, 16, 16], U32); mxf = S("mxf", [P, 16, 16])
            cand = S("cand", [P, 256]); cwk = S("cwk", [P, 256])
            scv = S("scv", [P, 8, 16]); posu = S("posu", [P, 8, 16], U32); posf = S("posf", [P, P])
            af = S("af", [P, P]); bf_ = S("bf", [P, P]); e1 = S("e1", [P, P]); e2 = S("e2", [P, P]); eidf = S("eidf", [P, P])
            gt = S("gt", [P, 8, 16]); zs = S("zs", [P, 8])

            def TH(fn, *a, **kw):
                return lambda: fn(*a, **kw)

            def routing_thunks(t):
                k = t % 2
                th = []
                for s in range(16):
                    w_ = wk[s % 2]
                    th.append(TH(V, "max", [ssb], [mx], out=mx[:, s, 0:8], in_=ssb[:, s, :]))
                    th.append(TH(V, "max_index", [ssb, mx], [mxi], out=mxi[:, s, 0:8], in_max=mx[:, s, 0:8], in_values=ssb[:, s, :]))
                    th.append(TH(V, "match_replace", [ssb, mx], [w_], out=w_[:], in_to_replace=mx[:, s, 0:8],
                                 in_values=ssb[:, s, :], imm_value=-1e30))
                    th.append(TH(V, "max", [w_], [mx], out=mx[:, s, 8:16], in_=w_[:]))
                    th.append(TH(V, "max_index", [w_, mx], [mxi], out=mxi[:, s, 8:16], in_max=mx[:, s, 8:16], in_values=w_[:]))
                th.append(TH(V, "tensor_copy", [mxi], [mxf], out=mxf[:], in_=mxi[:]))
                for h in range(8):
                    c3 = cand[:].rearrange("p (a b) -> p a b", a=16)
                    th.append(TH(V, "tensor_tensor", [mx], [cand], out=c3, in0=bc(mx[:, 2 * h, :], [P, 16, 16], 2),
                                 in1=bc(mx[:, 2 * h + 1, :], [P, 16, 16], 1), op=ALU.add))
                    th.append(TH(V, "max", [cand], [scv], out=scv[:, h, 0:8], in_=cand[:]))
                    th.append(TH(V, "match_replace", [cand, scv], [cwk], out=cwk[:], in_to_replace=scv[:, h, 0:8],
                                 in_values=cand[:], imm_value=-1e30))
                    th.append(TH(V, "max", [cwk], [scv], out=scv[:, h, 8:16], in_=cwk[:]))
                    th.append(TH(V, "max_index", [cand, scv], [posu], out=posu[:, h, 0:8], in_max=scv[:, h, 0:8], in_values=cand[:]))
                    th.append(TH(V, "max_index", [cwk, scv], [posu], out=posu[:, h, 8:16], in_max=scv[:, h, 8:16], in_values=cwk[:]))
                E3 = E[:].rearrange("p (n a) -> p n a", a=16)
                E4 = E[:].rearrange("p (h k a) -> p h k a", h=8, k=16)
                th.append(TH(V, "tensor_copy", [posu], [posf], out=posf[:], in_=posu[:].rearrange("p h k -> p (h k)")))
                th.append(TH(V, "tensor_tensor", [posf, i16], [E], out=E3, in0=bc(posf[:], [P, P, 16], 2),
                             in1=bc(i16[:, 16:32], [P, P, 16], 1), op=ALU.is_ge))
                th.append(TH(V, "tensor_reduce", [E], [af], out=af[:], in_=E3, axis=AX.X, op=ALU.add))
                th.append(TH(V, "scalar_tensor_tensor", [af, posf], [bf_], out=bf_[:], in0=af[:], scalar=-16.0, in1=posf[:],
                             op0=ALU.mult, op1=ALU.add))
                for (src, par, dst) in ((af, 0, e1), (bf_, 1, e2)):
                    th.append(TH(V, "tensor_tensor", [src, i16], [E], out=E3, in0=bc(src[:], [P, P, 16], 2),
                                 in1=bc(i16[:, 0:16], [P, P, 16], 1), op=ALU.is_equal))
                    th.append(TH(V, "tensor_tensor", [E, mxf], [E], out=E4, in0=E4,
                                 in1=bc(mxf[:, par:16:2, :], [P, 8, 16, 16], 2), op=ALU.mult))
                    th.append(TH(V, "tensor_reduce", [E], [dst], out=dst[:], in_=E3, axis=AX.X, op=ALU.add))
                th.append(TH(V, "scalar_tensor_tensor", [e1, e2], [eidf], out=eidf[:], in0=e1[:], scalar=128.0, in1=e2[:],
                             op0=ALU.mult, op1=ALU.add))
                th.append(TH(V, "tensor_scalar", [eidf], [eidf], out=eidf[:], in0=eidf[:], scalar1=float(NEXP - 1), scalar2=0.0,
                             op0=ALU.min, op1=ALU.max))
                th.append(TH(V, "tensor_copy", [eidf], [idx[k]], out=idx[k][:], in_=eidf[:]))
                th.append(TH(V, "tensor_tensor", [scv], [gt], out=gt[:], in0=scv[:], in1=bc(scv[:, :, 0], [P, 8, 16], 2), op=ALU.subtract))
                th.append(TH(A, "activation", [gt], [gt], out=gt[:], in_=gt[:], func=AF.Exp))
                th.append(TH(V, "tensor_reduce", [gt], [zs], out=zs[:], in_=gt[:], axis=AX.X, op=ALU.add))
                th.append(TH(V, "reciprocal", [zs], [zs], out=zs[:], in_=zs[:]))
                th.append(TH(V, "tensor_tensor", [gt, zs], [gate[k]], out=gate[k][:].rearrange("p (h k) -> p h k", h=8), in0=gt[:],
                             in1=bc(zs[:], [P, 8, 16], 2), op=ALU.mult))
                return th

            def prologue(t):
                k = t % 2
                tsl = slice(t * P, (t + 1) * P)
                D("sp", x1t[k][:], x1_d[tsl, :], [x1_d], [x1t[k]])
                D("sp", xnb[k][:], xnb_d[tsl, :], [xnb_d], [xnb[k]])

            def epilogue(t):
                k = t % 2
                tsl = slice(t * P, (t + 1) * P)
                x2 = x1t[k]
                for n in range(4):
                    nsl = slice(n * 512, (n + 1) * 512)
                    V("tensor_tensor", [accp[n], x2], [x2], out=x2[:, nsl], in0=accp[n][:], in1=x2[:, nsl], op=ALU.add)
                A("activation", [x2], [junka, ss3], out=junka[:], in_=x2[:], func=AF.Square, accum_out=ss3[:])
                rstd_from_ss(ss3, rsf3, DM)
                V("scalar_tensor_tensor", [x2, rsf3, gbl], [x2], out=x2[:], in0=x2[:], scalar=rsf3[:, 0:1], in1=gbl[:],
                  op0=ALU.mult, op1=ALU.mult)
                D("sp", out_d[tsl, :], x2[:], [x2], [out_d])

            def gather_thunk(G, j):
                t, g = divmod(G, NGT)
                k = t % 2
                s_ = g * GS + j
                slot = uv[(G * GS + j) % NR]
                GATHER(slot[:], uvb_d[:, :], idx[k][:, s_:s_ + 1], [idx[k], uvb_d], [slot])

            def dot_thunk(G, j):
                t, g = divmod(G, NGT)
                k = t % 2
                slot = uv[(G * GS + j) % NR]
                a_ = a4[G % 3]
                if j == 0:
                    V("memset", [], [a_], ap=a_[:], constant=0.0)
                V("scalar_tensor_tensor", [xnb[k]] + ([a_] if j == 0 else []), [slot] + ([a_] if j == GS - 1 else []),
                  out=slot[:, 0:DM], in0=slot[:, 0:DM], scalar=1.0, in1=xnb[k][:], op0=ALU.mult, op1=ALU.mult,
                  accum_out=a_[:, j:j + 1])

            def gelu_thunk(G, j):
                t, g = divmod(G, NGT)
                k = t % 2
                a_ = a4[G % 3]; q_ = tq[G % 2]; c_ = cf[G % 3]
                if j == 0:
                    A("activation", [a_], [q_], out=q_[:], in_=a_[:], func=AF.Gelu_apprx_tanh)
                elif j == 1:
                    V("tensor_tensor", [q_, gate[k]], [c_], out=c_[:], in0=q_[:], in1=gate[k][:, g * GS:(g + 1) * GS], op=ALU.mult)

            ndg = [0]

            def mm_thunk(G, j):
                t, g = divmod(G, NGT)
                s_ = g * GS + j
                slot = uv[(G * GS + j) % NR]
                c_ = cf[G % 3]
                d_ = dg[ndg[0] % 4]; ndg[0] += 1
                A("activation", [c_, identf], [d_], out=d_[:], in_=identf[:], func=AF.Copy, scale=c_[:, j:j + 1])
                for n in range(4):
                    M([d_, slot], [accp[n]], out=accp[n][:], lhsT=d_[:], rhs=slot[:, DM + n * 512:DM + (n + 1) * 512],
                      start=(s_ == 0), stop=(s_ == P - 1))

            D("sp", ssb[:].rearrange("p c t -> p (c t)"), sc_d[0], [sc_d], [ssb])
            for f_ in routing_thunks(0):
                f_()
            prologue(0)
            for G0 in range(2):
                for j in range(GS):
                    gather_thunk(G0, j)
            rt = []
            per_slot = 0
            for G in range(NG + 2):
                t, g = divmod(G, NGT)
                if G < NG and g == 0 and t + 1 < NT_OWN:
                    D("sp", ssb[:].rearrange("p c t -> p (c t)"), sc_d[t + 1], [sc_d], [ssb])
                    rt = routing_thunks(t + 1)
                    per_slot = -(-len(rt) // ((NGT - 6) * GS))
                if G < NG and g == NGT // 2 and t + 1 < NT_OWN:
                    prologue(t + 1)
                if G < NG and g == NGT - 4:
                    while rt:
                        rt.pop(0)()
                if 0 <= G - 1 < NG:
                    gelu_thunk(G - 1, 0)
                for j in range(GS):
                    if 0 <= G - 2 < NG:
                        mm_thunk(G - 2, j)
                    if G + 2 < NG:
                        gather_thunk(G + 2, j)
                    if G < NG:
                        dot_thunk(G, j)
                    if j == 1 and 0 <= G - 1 < NG:
                        gelu_thunk(G - 1, 1)
                    for _ in range(per_slot):
                        if rt:
                            rt.pop(0)()
                if 0 <= G - 2 < NG and (G - 2) % NGT == NGT - 1:
                    epilogue((G - 2) // NGT)
            pg.barrier()
            pg.emit()
            if stop_after == "D2":
                return nc, pg
    return nc, pg


def _pcol(v, n):
    return np.ascontiguousarray(np.asarray(v, np.float32).reshape(n, P).T)


def _blockdiag(w):
    out = np.zeros((8, P, P), np.float32)
    for c in range(8):
        out[c, 0:64, 0:64] = w[2 * c]
        out[c, 64:128, 64:128] = w[2 * c + 1]
    return out


def make_in_maps(inp):
    f32 = np.float32
    x = np.asarray(inp["x"], f32)
    L = 0
    half_d = 32
    inv = (np.float32(10000.0) ** (-np.arange(half_d, dtype=f32) / np.float32(half_d))).astype(f32)
    ident = np.eye(P, dtype=f32)
    jj = np.arange(P)[:, None]
    qq = np.arange(P)[None, :]
    mask_prev = (jj >= qq).astype(f32)
    mask_next = (jj <= qq).astype(f32)
    common = {
        "w_in": np.ascontiguousarray(inp["w_in"][L], f32),
        "w_out": np.ascontiguousarray(inp["w_out"][L], f32),
        "w_pq": np.ascontiguousarray(inp["w_pq"][L], f32),
        "u_emb": np.ascontiguousarray(inp["u_emb"][L], f32),
        "v_emb": np.ascontiguousarray(inp["v_emb"][L], f32),
        "ident": ident, "mask_prev": mask_prev, "mask_next": mask_next,
        "iota16": np.ascontiguousarray(np.broadcast_to(np.concatenate([np.arange(16, dtype=f32), np.array([16 * m for m in range(1, 16)] + [1e9], f32)])[None, :], (P, 32))),
        "iota256": np.ascontiguousarray(np.broadcast_to(np.arange(256, dtype=f32)[None, :], (P, 256))),
        "gm": _pcol(inp["g_mix"][L], 16),
        "gao": _pcol(inp["g_attn_out"][L], 8),
        "glo": _pcol(inp["g_lru_out"][L], 8),
        "sinkb": np.ascontiguousarray(np.broadcast_to(np.asarray(inp["sink"][L], f32)[None, :], (P, 16))),
        "cvb": _pcol(inp["conv_b"][L], 8),
        "gffn_b": np.ascontiguousarray(np.broadcast_to(np.asarray(inp["g_ffn"][L], f32)[None, :], (P, DM))),
        "gfinal_b": np.ascontiguousarray(np.broadcast_to(np.asarray(inp["g_final"], f32)[None, :], (P, DM))),
    }
    sk = np.stack([np.asarray(inp["sub_k1"][L], f32), np.asarray(inp["sub_k2"][L], f32)], axis=1)
    common["subkT"] = np.ascontiguousarray(sk.reshape(16, P, P).transpose(2, 0, 1))
    cw = np.asarray(inp["conv_w"][L], f32)
    dirs = {
        "fwd": [inp["fwd_wa"][L], inp["fwd_wx"][L], inp["fwd_ba"][L], inp["fwd_bx"][L], inp["fwd_lam"][L]],
        "bwd": [inp["bwd_wa"][L], inp["bwd_wx"][L], inp["bwd_ba"][L], inp["bwd_bx"][L], inp["bwd_lam"][L]],
    }
    maps = []
    for c in range(8):
        b, hf = divmod(c, 2)
        xl = x[b] if hf == 0 else x[b, ::-1]
        pos = np.arange(17 * P) if hf == 0 else (SEQ - 1 - np.arange(17 * P))
        ang = pos.astype(f32)[:, None] * inv[None, :]
        cos = np.cos(ang).astype(f32)
        sin = np.sin(ang).astype(f32)
        w5 = np.zeros((5, 1024), f32)
        if hf == 0:
            w5[0:4] = cw
        else:
            for j in range(4):
                w5[4 - j] = cw[j]
        w5p = np.stack([_pcol(w5[o], 8) for o in range(5)], axis=2).reshape(P, 40)
        d1, d2 = (dirs["fwd"], dirs["bwd"]) if hf == 0 else (dirs["bwd"], dirs["fwd"])
        gatew = np.stack([_blockdiag(np.asarray(d1[0], f32)), _blockdiag(np.asarray(d1[1], f32)),
                          _blockdiag(np.asarray(d2[0], f32)), _blockdiag(np.asarray(d2[1], f32))], axis=0)
        gateb = np.concatenate([_pcol(d1[2], 8), _pcol(d1[3], 8), _pcol(d2[2], 8), _pcol(d2[3], 8)], axis=1)
        lam = np.concatenate([_pcol(d1[4], 8), _pcol(d2[4], 8)], axis=1)
        m = dict(common)
        m.update({
            "x": np.ascontiguousarray(xl),
            "rope_cs": np.ascontiguousarray(np.concatenate([cos, cos], axis=1)),
            "rope_sn": np.ascontiguousarray(np.concatenate([-sin, sin], axis=1)),
            "w5": np.ascontiguousarray(w5p),
            "gatew": np.ascontiguousarray(gatew),
            "gateb": np.ascontiguousarray(gateb),
            "lam": np.ascontiguousarray(lam),
        })
        maps.append(m)
    return maps


_CACHE = {}


def kernel(**inputs):
    if "nc" not in _CACHE:
        _CACHE["nc"] = build_program()[0]
    nc = _CACHE["nc"]
    maps = make_in_maps(inputs)
    res = run_bass_kernel_spmd(nc, maps, core_ids=list(range(8)))
    out = np.zeros((4, SEQ, DM), np.float32)
    for c in range(8):
        b, hf = divmod(c, 2)
        o = np.asarray(res.results[c]["out"], np.float32)
        if hf == 0:
            out[b, 0:OWN] = o
        else:
            out[b, OWN:SEQ] = o[::-1]
    return out
```

```python
import contextlib
import numpy as np
import concourse.bass as bass
import concourse.mybir as mybir
from concourse.bass_utils import run_bass_kernel_spmd

F32 = mybir.dt.float32
BF16 = mybir.dt.bfloat16
I32 = mybir.dt.int32
U32 = mybir.dt.uint32
ALU = mybir.AluOpType
AF = mybir.ActivationFunctionType
AX = mybir.AxisListType

P = 128
DM = 2048
SEQ = 4096
OWN = 2048
NT_OWN = 16
NT_ALL = 32
INC = 3584
EPS = 1e-6
NEXP = 16384
EPOCH = 8000
DMA_SEM_MAX = 30000
GELU_C = 1.5957691216057308


class Buf:
    __slots__ = ("name", "last_write", "reads", "dsem", "dcount")

    def __init__(self, name):
        self.name = name
        self.last_write = None
        self.reads = {}
        self.dsem = None
        self.dcount = 0


class Prog:
    ENG = ("pe", "act", "dve", "pool", "sp")

    def __init__(self, nc):
        self.nc = nc
        self.q = {e: [] for e in self.ENG}
        self.cnt = {e: 0 for e in self.ENG}
        self.esems = {}
        self.seen = {e: {} for e in self.ENG}
        self.sem_objs = {}
        self.nsem = 0
        self.latest = {}
        self.same_engine_sync = {"pe": False, "act": True, "dve": True, "pool": True, "sp": False}
        self.ninst = 0

    def _new_sem(self, name):
        h = self.nc.alloc_semaphore(name=name)
        self.nsem += 1
        key = self.nsem
        self.sem_objs[key] = h
        return key

    def _esem(self, eng, epoch):
        k = (eng, epoch)
        if k not in self.esems:
            self.esems[k] = self._new_sem(f"e_{eng}_{epoch}")
        return self.esems[k]

    def _deps(self, eng, reads, writes):
        toks = []
        for b in reads:
            if b.last_write is not None:
                toks.append(b.last_write)
        for b in writes:
            if b.last_write is not None:
                toks.append(b.last_write)
            toks.extend(b.reads.values())
        need = {}
        for (semkey, val, teng) in toks:
            if teng == eng and not self.same_engine_sync[eng]:
                continue
            if self.seen[eng].get(semkey, 0) >= val:
                continue
            if need.get(semkey, 0) < val:
                need[semkey] = val
        for semkey, val in need.items():
            self.seen[eng][semkey] = val
        return list(need.items())

    def _record(self, tok, reads, writes):
        for b in reads:
            old = b.reads.get(tok[0])
            if old is None or old[1] < tok[1]:
                b.reads[tok[0]] = tok
        for b in writes:
            b.last_write = tok
            b.reads = {}
        self.latest[tok[0]] = tok

    def op(self, eng, fn, reads=(), writes=()):
        waits = self._deps(eng, reads, writes)
        self.cnt[eng] += 1
        epoch, val = divmod(self.cnt[eng] - 1, EPOCH)
        semkey = self._esem(eng, epoch)
        tok = (semkey, val + 1, eng)
        self.q[eng].append((fn, waits, (semkey, 1)))
        self._record(tok, reads, writes)
        self.ninst += 1

    def dma(self, eng, fn, reads=(), writes=()):
        waits = self._deps(eng, reads, writes)
        b = writes[0]
        if b.dsem is None or b.dcount + 16 > DMA_SEM_MAX:
            b.dsem = self._new_sem(f"d_{b.name}")
            b.dcount = 0
        b.dcount += 16
        tok = (b.dsem, b.dcount, "dma")
        self.q[eng].append((fn, waits, (b.dsem, 16)))
        self._record(tok, reads, writes)
        self.ninst += 1

    def barrier(self):
        for eng in self.ENG:
            need = []
            for semkey, (sk, val, teng) in self.latest.items():
                if self.seen[eng].get(semkey, 0) >= val:
                    continue
                if teng == eng and eng in ("pe", "sp"):
                    pass
                self.seen[eng][semkey] = val
                need.append((semkey, val))
            if need:
                self.q[eng].append((None, need, None))

    def emit(self):
        nc = self.nc
        objs = self.sem_objs
        qs = self.q
        self.q = {e: [] for e in self.ENG}

        def run(engname):
            def body(e):
                for fn, waits, inc in qs[engname]:
                    for semkey, val in waits:
                        e.wait_ge(objs[semkey], val)
                    if fn is not None:
                        fn(e).then_inc(objs[inc[0]], inc[1])
            return body

        with nc.Block() as block:
            block.tensor(run("pe"))
            block.scalar(run("act"))
            block.vector(run("dve"))
            block.gpsimd(run("pool"))
            block.sync(run("sp"))


class T:
    def __init__(self, t, name, psum=False):
        self.t = t
        self.b = Buf(name)
        self.psum = psum

    def __getitem__(self, k):
        return self.t[k]


def build_program(debug=False, stop_after=None, only=None):
    nc = bass.Bass("TRN2", target_bir_lowering=False)
    pg = Prog(nc)

    def din(name, shape, dt=F32):
        return nc.dram_tensor(name, list(shape), dt, kind="ExternalInput").ap()

    def dscr(name, shape, dt=F32):
        kind = "ExternalOutput" if debug else "Internal"
        return T(nc.dram_tensor(name, list(shape), dt, kind=kind).ap(), name)

    x_d = din("x", [SEQ, DM])
    w_in_d = din("w_in", [DM, INC])
    w_out_d = din("w_out", [DM, DM])
    w_pq_d = din("w_pq", [DM, DM])
    u_d = din("u_emb", [NEXP, DM])
    v_d = din("v_emb", [NEXP, DM])
    cs_d = din("rope_cs", [17 * P, 64])
    sn_d = din("rope_sn", [17 * P, 64])
    ident_d = din("ident", [P, P])
    mprev_d = din("mask_prev", [P, P])
    mnext_d = din("mask_next", [P, P])
    gm_d = din("gm", [P, 16])
    gao_d = din("gao", [P, 8])
    glo_d = din("glo", [P, 8])
    sink_d = din("sinkb", [P, 16])
    w5_d = din("w5", [P, 8 * 5])
    cvb_d = din("cvb", [P, 8])
    gw_d = din("gatew", [4, 8, P, P])
    gb_d = din("gateb", [P, 4 * 8])
    lam_d = din("lam", [P, 2 * 8])
    gbf_d = din("gffn_b", [P, DM])
    gbl_d = din("gfinal_b", [P, DM])
    subk_d = din("subkT", [P, 16, P])
    iota_d = din("iota256", [P, 256])
    i16_d = din("iota16", [P, 32])
    out_d = T(nc.dram_tensor("out", [OWN, DM], F32, kind="ExternalOutput").ap(), "out")

    xrT_d = dscr("xrT", [8, P, SEQ])
    xgT_d = dscr("xgT", [8, P, OWN])
    yaT_d = dscr("yaT", [NT_OWN, P, 1024], BF16)
    rsa_d = dscr("rsa", [NT_OWN, P, 1])
    ylT_d = dscr("ylT", [8, P, OWN], BF16)
    ysq_d = dscr("ysq", [8, P, OWN], BF16)
    x1_d = dscr("x1", [OWN, DM])
    idx_d = dscr("eidx", [NT_OWN, P, P], I32)
    gate_d = dscr("gate", [NT_OWN, P, P])
    uvb_d = dscr("uvb", [NEXP, 2 * DM], BF16)
    xnb_d = dscr("xnbs", [OWN, DM], BF16)
    sc_d = dscr("scs", [NT_OWN, P, 16 * P])

    def OP(eng, method, reads, writes, **kw):
        rd = [r.b for r in reads if not r.psum]
        wr = [w.b for w in writes] + [r.b for r in reads if r.psum]
        pg.op(eng, lambda e: getattr(e, method)(**kw), rd, wr)

    def V(method, reads, writes, **kw):
        OP("dve", method, reads, writes, **kw)

    def A(method, reads, writes, **kw):
        OP("act", method, reads, writes, **kw)

    def G(method, reads, writes, **kw):
        OP("pool", method, reads, writes, **kw)

    def M(reads, writes, **kw):
        OP("pe", "matmul", reads, writes, **kw)

    def TR(reads, writes, **kw):
        OP("pe", "transpose", reads, writes, **kw)

    def D(eng, out, in_, reads, writes):
        pg.dma(eng, lambda e: e.dma_start(out=out, in_=in_), [r.b for r in reads], [w.b for w in writes])

    def GATHER(out, tab, off, reads, writes):
        pg.dma("pool", lambda e: e.indirect_dma_start(
            out=out, out_offset=None, in_=tab, in_offset=bass.IndirectOffsetOnAxis(ap=off, axis=0)),
            [r.b for r in reads], [w.b for w in writes])

    def bc(ap, shape, axis):
        return ap.unsqueeze(axis).to_broadcast(list(shape))

    gstack = contextlib.ExitStack()

    def mk(stack):
        def S(name, shape, dt=F32):
            return T(stack.enter_context(nc.sbuf_tensor("s_" + name, list(shape), dt)), name)

        def PS(name, shape, dt=F32):
            return T(stack.enter_context(nc.psum_tensor("p_" + name, list(shape), dt)), name, psum=True)
        return S, PS

    with gstack:
        GS, _ = mk(gstack)
        identf = GS("identf", [P, P])
        identb = GS("identb", [P, P], BF16)
        onesb = GS("onesb", [P, 1], BF16)
        onesf = GS("onesf", [P, 1])
        epsb = GS("epsb", [P, 1])
        D("sp", identf[:], ident_d, [], [identf])
        V("tensor_copy", [identf], [identb], out=identb[:], in_=identf[:])
        V("memset", [], [onesb], ap=onesb[:], constant=1.0)
        V("memset", [], [onesf], ap=onesf[:], constant=1.0)
        V("memset", [], [epsb], ap=epsb[:], constant=EPS)

        def rstd_from_ss(ss, out, n):
            A("activation", [ss, epsb], [out], out=out[:], in_=ss[:, 0:1], func=AF.Sqrt, scale=1.0 / n, bias=epsb[:])
            V("reciprocal", [out], [out], out=out[:], in_=out[:])

        with contextlib.ExitStack() as st:
            S, PS = mk(st)
            Win = [S(f"win{kc}", [P, INC], BF16) for kc in range(16)]
            gm = S("gm", [P, 16]); gao = S("gao", [P, 8]); es = S("es", [P, 16])
            mprev = S("mprev", [P, P], BF16); mnext = S("mnext", [P, P], BF16)
            mtmp = S("mtmp", [P, P])
            D("sp", gm[:], gm_d, [], [gm])
            D("sp", gao[:], gao_d, [], [gao])
            D("sp", es[:], sink_d, [], [es])
            A("activation", [es], [es], out=es[:], in_=es[:], func=AF.Exp)
            D("sp", mtmp[:], mprev_d, [], [mtmp])
            V("tensor_copy", [mtmp], [mprev], out=mprev[:], in_=mtmp[:])
            D("sp", mtmp[:], mnext_d, [], [mtmp])
            V("tensor_copy", [mtmp], [mnext], out=mnext[:], in_=mtmp[:])
            for kc in range(16):
                D("pool", Win[kc][:], w_in_d[kc * P:(kc + 1) * P, :], [], [Win[kc]])

            xt = [S(f"xt{i}", [P, DM]) for i in range(2)]
            junkb = S("junkb", [P, DM], BF16)
            ss = S("ss", [P, 1]); rstd = S("rstd", [P, 1])
            hbs = [S(f"hb{i}", [P, DM], BF16) for i in range(2)]
            hTs = [S(f"hT{i}", [P, 16, P], BF16) for i in range(2)]
            css = [S(f"cs{i}", [P, 64]) for i in range(2)]; sns = [S(f"sn{i}", [P, 64]) for i in range(2)]
            cs = css[0]; sn = sns[0]
            t1 = S("t1", [P, 512]); t2 = S("t2", [P, 512])
            qb = S("qb", [P, 1024], BF16); kb = S("kb", [P, 256], BF16)
            qT = [S(f"qT{i}", [64, 16 * P], BF16) for i in range(2)]
            kT = [S(f"kT{i}", [64, 4 * P], BF16) for i in range(4)]
            va = [S(f"va{i}", [P, 4, 65], BF16) for i in range(4)]
            Pb = [S(f"Pb{i}", [P, 512], BF16) for i in range(3)]
            den = S("den", [P, 4])
            yat = S("yat", [P, 16, 64]); yab = S("yab", [P, 1024], BF16)
            yaT = S("yaT", [P, 8, P], BF16)
            ssa = S("ssa", [P, 1]); rsa = S("rsa", [P, 1])
            fm = [S(f"fm{i}", [P, 4, P]) for i in range(2)]
            pj = [PS(f"pj{i}", [P, 512]) for i in range(2)]
            pT = PS("pT", [P, DM], BF16)
            scp = [PS(f"scp{i}", [P, 512]) for i in range(3)]
            pv = PS("pv", [P, 512])
            for i in range(4):
                V("memset", [], [va[i]], ap=va[i][:], constant=1.0)

            def rope(src, nh, dst, srcT, dstT):
                w = nh * 64
                s3 = src.rearrange("p (h d) -> p h d", h=nh)
                s4 = src.rearrange("p (h t d) -> p h t d", h=nh, t=2)[:, :, ::-1, :]
                V("tensor_tensor", [srcT, cs], [t1], out=t1[:, 0:w].rearrange("p (h d) -> p h d", h=nh),
                  in0=s3, in1=bc(cs[:], [P, nh, 64], 1), op=ALU.mult)
                V("tensor_tensor", [srcT, sn], [t2], out=t2[:, 0:w].rearrange("p (h t d) -> p h t d", h=nh, t=2),
                  in0=s4, in1=bc(sn[:].rearrange("p (t d) -> p t d", t=2), [P, nh, 2, 32], 1), op=ALU.mult)
                G("tensor_tensor", [t1, t2], [dstT], out=dst, in0=t1[:, 0:w], in1=t2[:, 0:w], op=ALU.add)

            def attention(t, fm_thunks=()):
                fm_thunks = list(fm_thunks)
                blocks = [b for b in (t - 1, t, t + 1) if b >= 0]
                qTt = qT[t % 2]
                for g in range(4):
                    for bi, b in enumerate(blocks):
                        M([kT[b % 4], qTt], [scp[bi]], out=scp[bi][:], lhsT=kT[b % 4][:, g * P:(g + 1) * P],
                          rhs=qTt[:, 4 * g * P:(4 * g + 4) * P], start=True, stop=True)
                        A("activation", [scp[bi]], [Pb[bi]], out=Pb[bi][:], in_=scp[bi][:], func=AF.Exp, scale=0.125)
                        if b != t:
                            mk_ = mprev if b == t - 1 else mnext
                            v3 = Pb[bi][:].rearrange("p (r q) -> p r q", r=4)
                            G("tensor_tensor", [Pb[bi], mk_], [Pb[bi]], out=v3, in0=v3,
                              in1=bc(mk_[:], [P, 4, P], 1), op=ALU.mult)
                    if fm_thunks:
                        fm_thunks.pop(0)()
                    for r in range(4):
                        for bi, b in enumerate(blocks):
                            M([Pb[bi], va[b % 4]], [pv], out=pv[:, r * 65:(r + 1) * 65], lhsT=Pb[bi][:, r * P:(r + 1) * P],
                              rhs=va[b % 4][:, g, :], start=(bi == 0), stop=(bi == len(blocks) - 1))
                    pv3 = pv[:, 0:260].rearrange("p (r d) -> p r d", r=4)
                    V("tensor_tensor", [pv, es], [den], out=den[:], in0=pv3[:, :, 64], in1=es[:, 4 * g:4 * g + 4], op=ALU.add)
                    V("reciprocal", [den], [den], out=den[:], in_=den[:])
                    V("tensor_tensor", [pv, den], [yat], out=yat[:, 4 * g:4 * g + 4, :], in0=pv3[:, :, 0:64],
                      in1=bc(den[:], [P, 4, 64], 2), op=ALU.mult)
                while fm_thunks:
                    fm_thunks.pop(0)()
                yflat = yat[:].rearrange("p h d -> p (h d)")
                A("activation", [yat], [junkb, ssa], out=junkb[:, 0:1024], in_=yflat, func=AF.Square, accum_out=ssa[:])
                rstd_from_ss(ssa, rsa, 1024)
                D("sp", rsa_d[t], rsa[:], [rsa], [rsa_d])
                A("activation", [yat], [yab], out=yab[:], in_=yflat, func=AF.Copy)
                for kc in range(8):
                    TR([yab, identb], [pT], out=pT[:, kc * P:(kc + 1) * P], in_=yab[:, kc * P:(kc + 1) * P], identity=identb[:])
                V("tensor_tensor", [pT, gao], [yaT], out=yaT[:], in0=pT[:, 0:1024].rearrange("p (k t) -> p k t", k=8),
                  in1=bc(gao[:], [P, 8, P], 2), op=ALU.mult)
                D("sp", yaT_d[t], yaT[:].rearrange("p k t -> p (k t)"), [yaT], [yaT_d])

            _AT = NT_ALL
            def load_x(i):
                D("sp", xt[i % 2][:], x_d[i * P:(i + 1) * P, :], [], [xt[i % 2]])
                if i <= NT_OWN:
                    D("sp", css[i % 2][:], cs_d[i * P:(i + 1) * P, :], [], [css[i % 2]])
                    D("sp", sns[i % 2][:], sn_d[i * P:(i + 1) * P, :], [], [sns[i % 2]])

            def front(i):
                A("activation", [xt[i % 2]], [junkb, ss], out=junkb[:], in_=xt[i % 2][:], func=AF.Square, accum_out=ss[:])
                rstd_from_ss(ss, rstd, DM)
                A("activation", [xt[i % 2], rstd], [hbs[i % 2]], out=hbs[i % 2][:], in_=xt[i % 2][:], func=AF.Copy, scale=rstd[:])

            for i in range(_AT):
                own = i < NT_OWN
                hb = hbs[i % 2]; hT = hTs[i % 2]
                _tab, _c0 = (u_d, 0) if i < 16 else (v_d, DM)
                _r0 = (i % 16) * 1024
                D("pool", uvb_d[_r0:_r0 + 1024, _c0:_c0 + DM], _tab[_r0:_r0 + 1024, :], [], [uvb_d])
                x_t = xt[i % 2]
                cs = css[i % 2]; sn = sns[i % 2]
                if i == 0:
                    load_x(0)
                if i + 1 < _AT:
                    load_x(i + 1)
                if i == 0:
                    front(0)
                for kc in range(16):
                    TR([hb, identb], [pT], out=pT[:, kc * P:(kc + 1) * P], in_=hb[:, kc * P:(kc + 1) * P], identity=identb[:])
                V("tensor_tensor", [pT, gm], [hT], out=hT[:], in0=pT[:].rearrange("p (k t) -> p k t", k=16),
                  in1=bc(gm[:], [P, 16, P], 2), op=ALU.mult)
                if i <= NT_OWN:
                    pk = pj[0]
                    for kc in range(16):
                        M([hT, Win[kc]], [pk], out=pk[:], lhsT=hT[:, kc, :], rhs=Win[kc][:, 1024:1536],
                          start=(kc == 0), stop=(kc == 15))
                    rope(pk[:, 0:256], 4, kb[:], pk, kb)
                    A("activation", [pk], [va[i % 4]], out=va[i % 4][:, :, 0:64],
                      in_=pk[:, 256:512].rearrange("p (g d) -> p g d", g=4), func=AF.Copy)
                if own:
                    for grp in range(2):
                        pq = pj[1 - grp]
                        for kc in range(16):
                            M([hT, Win[kc]], [pq], out=pq[:], lhsT=hT[:, kc, :], rhs=Win[kc][:, grp * 512:(grp + 1) * 512],
                              start=(kc == 0), stop=(kc == 15))
                        rope(pq[:], 8, qb[:, grp * 512:(grp + 1) * 512], pq, qb)
                if i <= NT_OWN:
                    for g in range(4):
                        TR([kb, identb], [pT], out=pT[0:64, g * P:(g + 1) * P], in_=kb[:, g * 64:(g + 1) * 64], identity=identb[:])
                    V("tensor_copy", [pT], [kT[i % 4]], out=kT[i % 4][:], in_=pT[0:64, 0:512])
                if i + 1 < _AT:
                    front(i + 1)
                jobs = []
                if own:
                    jobs.append((1536, xgT_d))
                jobs.append((2560, xrT_d))
                fm_thunks = []
                nb = 0
                for col0, dst in jobs:
                    for c4 in range(2):
                        def fm_chunk(col0=col0, dst=dst, c4=c4, nb=nb, hT=hT, i=i):
                            pq = pj[nb % 2]; f = fm[nb % 2]
                            for cc in range(4):
                                c = c4 * 4 + cc
                                for kc in range(16):
                                    M([hT, Win[kc]], [pq], out=pq[:, cc * P:(cc + 1) * P],
                                      lhsT=Win[kc][:, col0 + c * P:col0 + (c + 1) * P], rhs=hT[:, kc, :],
                                      start=(kc == 0), stop=(kc == 15))
                            A("activation", [pq], [f], out=f[:].rearrange("p c t -> p (c t)"), in_=pq[:], func=AF.Copy)
                            D("sp", dst[c4 * 4:(c4 + 1) * 4, :, i * P:(i + 1) * P].rearrange("c p t -> p c t"), f[:], [f], [dst])
                        fm_thunks.append(fm_chunk)
                        nb += 1
                if 1 <= i <= NT_OWN:
                    attention(i - 1, fm_thunks)
                else:
                    for f_ in fm_thunks:
                        f_()
                if own:
                    qTt = qT[i % 2]
                    for h in range(16):
                        TR([qb, identb], [pT], out=pT[0:64, h * P:(h + 1) * P], in_=qb[:, h * 64:(h + 1) * 64], identity=identb[:])
                    V("tensor_copy", [pT], [qTt], out=qTt[:], in_=pT[0:64, :])
            pg.barrier()
            pg.emit()
            if stop_after == "A":
                return nc, pg

        with contextlib.ExitStack() as st:
            S, PS = mk(st)
            w5 = S("w5", [P, 40]); cvb = S("cvb", [P, 8]); gb = S("gb", [P, 32]); lam = S("lam", [P, 16])
            kap = S("kap", [P, 16]); kap2 = S("kap2", [P, 16]); glo = S("glo", [P, 8])
            gw = [[S(f"gw{k}_{c}", [P, P], BF16) for c in range(8)] for k in range(4)]
            for a_, d_ in ((w5, w5_d), (cvb, cvb_d), (gb, gb_d), (lam, lam_d), (glo, glo_d)):
                D("sp", a_[:], d_, [], [a_])
            for k in range(4):
                for c in range(8):
                    gw[k][c].b = gw[0][0].b
                    D("pool", gw[k][c][:], gw_d[k, c], [], [gw[k][c]])
            A("activation", [lam], [kap], out=kap[:], in_=lam[:], func=AF.Exp, scale=-1.0)
            A("activation", [kap, onesf], [kap], out=kap[:], in_=kap[:], func=AF.Ln, bias=onesf[:], scale=1.0)
            V("tensor_scalar", [kap], [kap2], out=kap2[:], in0=kap[:], scalar1=-16.0, scalar2=None, op0=ALU.mult)
            V("tensor_scalar", [kap], [kap], out=kap[:], in0=kap[:], scalar1=-8.0, scalar2=None, op0=ALU.mult)

            xrs = [S(f"xr{i}", [P, SEQ + 4]) for i in range(2)]; ccs = [S(f"cc{i}", [P, SEQ]) for i in range(2)]
            cbs = [S(f"cb{i}", [P, SEQ], BF16) for i in range(2)]
            Rb = [S(f"Rb{i}", [P, OWN]) for i in range(2)]; Ib = [S(f"Ib{i}", [P, OWN]) for i in range(2)]
            Qb = [S(f"Qb{i}", [P, OWN]) for i in range(2)]
            st_ = S("lru_state", [P, 1])
            h2 = S("h2", [P, OWN]); h1 = S("h1", [P, OWN])
            xg = S("xg", [P, OWN]); tg = S("tg", [P, OWN])
            ylb = S("ylb", [P, OWN], BF16); ysb = S("ysb", [P, OWN], BF16)
            pgt = [PS(f"pgt{i}", [P, 512]) for i in range(4)]
            for xr in xrs:
                V("memset", [], [xr], ap=xr[:, 0:2], constant=0.0)
                V("memset", [], [xr], ap=xr[:, SEQ + 2:SEQ + 4], constant=0.0)
            npg = 0
            njob = [0]

            def conv(c):
                xr = xrs[c % 2]; cc_ = ccs[c % 2]; cb = cbs[c % 2]
                V("tensor_scalar", [xr, w5, cvb], [cc_], out=cc_[:], in0=xr[:, 0:SEQ], scalar1=w5[:, c * 5:c * 5 + 1],
                  scalar2=cvb[:, c:c + 1], op0=ALU.mult, op1=ALU.add)
                for o in range(1, 5):
                    V("scalar_tensor_tensor", [xr, w5, cc_], [cc_], out=cc_[:], in0=xr[:, o:o + SEQ],
                      scalar=w5[:, c * 5 + o:c * 5 + o + 1], in1=cc_[:], op0=ALU.mult, op1=ALU.add)

            D("sp", xrs[0][:, 2:SEQ + 2], xrT_d[0], [xrT_d], [xrs[0]])
            D("sp", xg[:], xgT_d[0], [xgT_d], [xg])
            conv(0)
            for c in range(8):
                cc_ = ccs[c % 2]; cb = cbs[c % 2]
                if c + 1 < 8:
                    D("sp", xrs[(c + 1) % 2][:, 2:SEQ + 2], xrT_d[c + 1], [xrT_d], [xrs[(c + 1) % 2]])
                A("activation", [cc_], [cb], out=cb[:], in_=cc_[:], func=AF.Copy)
                for job, (dr, tok0, rev) in enumerate(((1, OWN, True), (1, 0, True), (0, 0, False))):
                    n_ = njob[0]; njob[0] += 1
                    R_ = Rb[n_ % 2]; I_ = Ib[n_ % 2]; Q_ = Qb[n_ % 2]
                    for tgp in range(OWN // 512):
                        sl = slice(tok0 + tgp * 512, tok0 + (tgp + 1) * 512)
                        dsl = slice(tgp * 512, (tgp + 1) * 512)
                        for (kk, dstt) in ((0, R_), (1, I_)):
                            k = dr * 2 + kk
                            pq = pgt[npg % 4]; npg += 1
                            M([gw[k][c], cb], [pq], out=pq[:], lhsT=gw[k][c][:], rhs=cb[:, sl], start=True, stop=True)
                            A("activation", [pq, gb], [dstt], out=dstt[:, dsl], in_=pq[:], func=AF.Sigmoid,
                              bias=gb[:, k * 8 + c:k * 8 + c + 1], scale=1.0)
                    kc_ = dr * 8 + c
                    A("activation", [R_, kap2], [Q_], out=Q_[:], in_=R_[:], func=AF.Exp, scale=kap2[:, kc_:kc_ + 1])
                    A("activation", [R_, kap], [R_], out=R_[:], in_=R_[:], func=AF.Exp, scale=kap[:, kc_:kc_ + 1])
                    A("activation", [Q_, onesf], [Q_], out=Q_[:], in_=Q_[:], func=AF.Sqrt, scale=-1.0, bias=onesf[:])
                    V("tensor_tensor", [Q_, I_], [I_], out=I_[:], in0=I_[:], in1=Q_[:], op=ALU.mult)
                    V("tensor_tensor", [cc_, I_], [I_], out=I_[:], in0=I_[:], in1=cc_[:, tok0:tok0 + OWN], op=ALU.mult)
                    if job == 0:
                        V("tensor_tensor_scan", [R_, I_], [Q_], out=Q_[:, ::-1], data0=R_[:, ::-1], data1=I_[:, ::-1],
                          initial=0.0, op0=ALU.mult, op1=ALU.add)
                        V("tensor_copy", [Q_], [st_], out=st_[:], in_=Q_[:, 0:1])
                        if c + 1 < 8:
                            conv(c + 1)
                    elif job == 1:
                        V("tensor_tensor_scan", [R_, I_, st_], [h2], out=h2[:, ::-1], data0=R_[:, ::-1], data1=I_[:, ::-1],
                          initial=st_[:, 0:1], op0=ALU.mult, op1=ALU.add)
                    else:
                        V("tensor_tensor_scan", [R_, I_], [h1], out=h1[:], data0=R_[:], data1=I_[:], initial=0.0,
                          op0=ALU.mult, op1=ALU.add)
                G("tensor_tensor", [h1, h2], [h1], out=h1[:], in0=h1[:], in1=h2[:], op=ALU.add)
                A("activation", [xg], [tg], out=tg[:], in_=xg[:], func=AF.Gelu_apprx_tanh)
                if c + 1 < 8:
                    D("sp", xg[:], xgT_d[c + 1], [xgT_d], [xg])
                V("tensor_tensor", [tg, h1], [tg], out=tg[:], in0=tg[:], in1=h1[:], op=ALU.mult)
                G("tensor_tensor", [tg], [ysb], out=ysb[:], in0=tg[:], in1=tg[:], op=ALU.mult)
                V("tensor_scalar", [tg, glo], [ylb], out=ylb[:], in0=tg[:], scalar1=glo[:, c:c + 1], scalar2=None, op0=ALU.mult)
                D("sp", ysq_d[c], ysb[:], [ysb], [ysq_d])
                D("sp", ylT_d[c], ylb[:], [ylb], [ylT_d])
            pg.barrier()
            pg.emit()
            if stop_after == "B":
                return nc, pg

        with contextlib.ExitStack() as st:
            S, PS = mk(st)
            Wo = [S(f"wo{kc}", [P, DM], BF16) for kc in range(16)]
            Wq = [S(f"wq{kc}", [P, DM], BF16) for kc in range(16)]
            for kc in range(16):
                Wo[kc].b = Wo[0].b
                D("pool", Wo[kc][:], w_out_d[kc * P:(kc + 1) * P, :], [], [Wo[kc]])
            for kc in range(16):
                Wq[kc].b = Wq[0].b
                D("pool", Wq[kc][:], w_pq_d[kc * P:(kc + 1) * P, :], [], [Wq[kc]])
            subk = S("subk", [P, 16, P]); gbf = S("gbf", [P, DM])
            D("sp", subk[:], subk_d, [], [subk])
            D("sp", gbf[:], gbf_d, [], [gbf])
            ya = S("ya", [P, 1024], BF16); yl = S("yl", [P, 8, P], BF16); ysq = S("ysqt", [P, 8, P], BF16)
            rsa = S("rsa_c", [P, 1]); rsl = S("rsl", [P, 1])
            xtt = S("xt_c", [P, DM]); x1s = [S(f"x1_c{i}", [P, DM]) for i in range(2)]
            ss = S("ss_c", [P, 1]); rsf = S("rsf_c", [P, 1])
            xnbs = [S(f"xnb_c{i}", [P, DM], BF16) for i in range(2)]; xnT = S("xnT_c", [P, 16, P], BF16)
            qTf = S("qTf", [P, 16, P]); ssb = S("ssb_c", [P, 16, P])
            pa = PS("pa", [P, 512]); pl = PS("pl", [P, 512]); pss = PS("pss", [P, 512])
            pT = PS("pT_c", [P, DM], BF16)
            pq2 = [PS(f"pq2_{i}", [P, 512]) for i in range(2)]
            psc = PS("psc", [P, 512])

            def loads_c(t):
                tsl = slice(t * P, (t + 1) * P)
                D("sp", ya[:], yaT_d[t], [yaT_d], [ya])
                D("sp", yl[:], ylT_d[:, :, tsl].rearrange("c p t -> p c t"), [ylT_d], [yl])
                D("sp", ysq[:], ysq_d[:, :, tsl].rearrange("c p t -> p c t"), [ysq_d], [ysq])
                D("sp", rsa[:], rsa_d[t], [rsa_d], [rsa])
                D("sp", xtt[:], x_d[tsl, :], [], [xtt])

            def stage_a(t):
                tsl = slice(t * P, (t + 1) * P)
                x1 = x1s[t % 2]; xnb = xnbs[t % 2]
                for c in range(8):
                    M([ysq, onesb], [pss], out=pss[:, 0:1], lhsT=ysq[:, c, :], rhs=onesb[:, 0:1], start=(c == 0), stop=(c == 7))
                rstd_from_ss(pss, rsl, 1024)
                for n in range(4):
                    nsl = slice(n * 512, (n + 1) * 512)
                    for kc in range(8):
                        M([ya, Wo[kc]], [pa], out=pa[:], lhsT=ya[:, kc * P:(kc + 1) * P], rhs=Wo[kc][:, nsl],
                          start=(kc == 0), stop=(kc == 7))
                    for kc in range(8):
                        M([yl, Wo[8 + kc]], [pl], out=pl[:], lhsT=yl[:, kc, :], rhs=Wo[8 + kc][:, nsl],
                          start=(kc == 0), stop=(kc == 7))
                    V("scalar_tensor_tensor", [pa, rsa, xtt], [x1], out=x1[:, nsl], in0=pa[:], scalar=rsa[:, 0:1],
                      in1=xtt[:, nsl], op0=ALU.mult, op1=ALU.add)
                    V("scalar_tensor_tensor", [pl, rsl, x1], [x1], out=x1[:, nsl], in0=pl[:], scalar=rsl[:, 0:1],
                      in1=x1[:, nsl], op0=ALU.mult, op1=ALU.add)
                if t + 1 < NT_OWN:
                    loads_c(t + 1)
                D("sp", x1_d[tsl, :], x1[:], [x1], [x1_d])
                A("activation", [x1], [xnb, ss], out=xnb[:], in_=x1[:], func=AF.Square, accum_out=ss[:])
                rstd_from_ss(ss, rsf, DM)
                V("scalar_tensor_tensor", [x1, rsf, gbf], [xnb], out=xnb[:], in0=x1[:], scalar=rsf[:, 0:1], in1=gbf[:],
                  op0=ALU.mult, op1=ALU.mult)
                D("sp", xnb_d[tsl, :], xnb[:], [xnb], [xnb_d])

            def stage_b(t):
                xnb = xnbs[t % 2]
                for kc in range(16):
                    TR([xnb, identb], [pT], out=pT[:, kc * P:(kc + 1) * P], in_=xnb[:, kc * P:(kc + 1) * P], identity=identb[:])
                A("activation", [pT], [xnT], out=xnT[:].rearrange("p k t -> p (k t)"), in_=pT[:], func=AF.Copy)
                for c4 in range(4):
                    pq = pq2[c4 % 2]
                    for cc in range(4):
                        c = c4 * 4 + cc
                        for kc in range(16):
                            M([Wq[kc], xnT], [pq], out=pq[:, cc * P:(cc + 1) * P], lhsT=Wq[kc][:, c * P:(c + 1) * P],
                              rhs=xnT[:, kc, :], start=(kc == 0), stop=(kc == 15))
                    A("activation", [pq], [qTf], out=qTf[:, c4 * 4:(c4 + 1) * 4, :].rearrange("p c t -> p (c t)"),
                      in_=pq[:], func=AF.Copy)
                for s4 in range(4):
                    for s_ in range(4):
                        s = s4 * 4 + s_
                        M([qTf, subk], [psc], out=psc[:, s_ * P:(s_ + 1) * P], lhsT=qTf[:, s, :], rhs=subk[:, s, :],
                          start=True, stop=True)
                    V("tensor_copy", [psc], [ssb], out=ssb[:, s4 * 4:(s4 + 1) * 4, :].rearrange("p c t -> p (c t)"), in_=psc[:])
                D("sp", sc_d[t], ssb[:].rearrange("p c t -> p (c t)"), [ssb], [sc_d])

            loads_c(0)
            stage_a(0)
            for t in range(NT_OWN):
                if t + 1 < NT_OWN:
                    stage_a(t + 1)
                stage_b(t)
            pg.barrier()
            pg.emit()
            if stop_after == "C":
                return nc, pg

        with contextlib.ExitStack() as st:
            S, PS = mk(st)
            NR = 16
            GS = 4
            NGT = P // GS
            NG = NT_OWN * NGT
            gbl = S("gbl", [P, DM]); i16 = S("i16", [P, 32])
            D("sp", gbl[:], gbl_d, [], [gbl])
            D("sp", i16[:], i16_d, [], [i16])
            x1t = [S(f"x1t2_{i}", [P, DM]) for i in range(2)]
            xnb = [S(f"xnb2_{i}", [P, DM], BF16) for i in range(2)]
            idx = [S(f"idx2_{i}", [P, P], I32) for i in range(2)]
            gate = [S(f"gate2_{i}", [P, P]) for i in range(2)]
            a4 = [S(f"a4_{i}", [P, GS]) for i in range(3)]
            tq = [S(f"tq4_{i}", [P, GS]) for i in range(2)]
            cf = [S(f"cf4_{i}", [P, GS]) for i in range(3)]
            junka = S("junka", [P, DM], BF16)
            ss3 = S("ss3", [P, 1]); rsf3 = S("rsf3", [P, 1])
            uv = [S(f"uv{i}", [P, 2 * DM], BF16) for i in range(NR)]
            dg = [S(f"dg{i}", [P, P], BF16) for i in range(4)]
            accp = [PS(f"accp{i}", [P, 512]) for i in range(4)]
            ssb = S("ssb_r", [P, 16, P]); E = S("E_r", [P, 2048]); wk = [S(f"wk{i}", [P, P]) for i in range(2)]
            mx = S("mx", [P, 16, 16]); mxi = S("mxi", [P, 16, 16], U32); mxf = S("mxf", [P, 16, 16])
            cand = S("cand", [P, 256]); cwk = S("cwk", [P, 256])
            scv = S("scv", [P, 8, 16]); posu = S("posu", [P, 8, 16], U32); posf = S("posf", [P, P])
            af = S("af", [P, P]); bf_ = S("bf", [P, P]); e1 = S("e1", [P, P]); e2 = S("e2", [P, P]); eidf = S("eidf", [P, P])
            gt = S("gt", [P, 8, 16]); zs = S("zs", [P, 8])

            def TH(fn, *a, **kw):
                return lambda: fn(*a, **kw)

            def routing_thunks(t):
                k = t % 2
                th = []
                for s in range(16):
                    w_ = wk[s % 2]
                    th.append(TH(V, "max", [ssb], [mx], out=mx[:, s, 0:8], in_=ssb[:, s, :]))
                    th.append(TH(V, "max_index", [ssb, mx], [mxi], out=mxi[:, s, 0:8], in_max=mx[:, s, 0:8], in_values=ssb[:, s, :]))
                    th.append(TH(V, "match_replace", [ssb, mx], [w_], out=w_[:], in_to_replace=mx[:, s, 0:8],
                                 in_values=ssb[:, s, :], imm_value=-1e30))
                    th.append(TH(V, "max", [w_], [mx], out=mx[:, s, 8:16], in_=w_[:]))
                    th.append(TH(V, "max_index", [w_, mx], [mxi], out=mxi[:, s, 8:16], in_max=mx[:, s, 8:16], in_values=w_[:]))
                th.append(TH(V, "tensor_copy", [mxi], [mxf], out=mxf[:], in_=mxi[:]))
                for h in range(8):
                    c3 = cand[:].rearrange("p (a b) -> p a b", a=16)
                    th.append(TH(V, "tensor_tensor", [mx], [cand], out=c3, in0=bc(mx[:, 2 * h, :], [P, 16, 16], 2),
                                 in1=bc(mx[:, 2 * h + 1, :], [P, 16, 16], 1), op=ALU.add))
                    th.append(TH(V, "max", [cand], [scv], out=scv[:, h, 0:8], in_=cand[:]))
                    th.append(TH(V, "match_replace", [cand, scv], [cwk], out=cwk[:], in_to_replace=scv[:, h, 0:8],
                                 in_values=cand[:], imm_value=-1e30))
                    th.append(TH(V, "max", [cwk], [scv], out=scv[:, h, 8:16], in_=cwk[:]))
                    th.append(TH(V, "max_index", [cand, scv], [posu], out=posu[:, h, 0:8], in_max=scv[:, h, 0:8], in_values=cand[:]))
                    th.append(TH(V, "max_index", [cwk, scv], [posu], out=posu[:, h, 8:16], in_max=scv[:, h, 8:16], in_values=cwk[:]))
                E3 = E[:].rearrange("p (n a) -> p n a", a=16)
                E4 = E[:].rearrange("p (h k a) -> p h k a", h=8, k=16)
                th.append(TH(V, "tensor_copy", [posu], [posf], out=posf[:], in_=posu[:].rearrange("p h k -> p (h k)")))
                th.append(TH(V, "tensor_tensor", [posf, i16], [E], out=E3, in0=bc(posf[:], [P, P, 16], 2),
                             in1=bc(i16[:, 16:32], [P, P, 16], 1), op=ALU.is_ge))
                th.append(TH(V, "tensor_reduce", [E], [af], out=af[:], in_=E3, axis=AX.X, op=ALU.add))
                th.append(TH(V, "scalar_tensor_tensor", [af, posf], [bf_], out=bf_[:], in0=af[:], scalar=-16.0, in1=posf[:],
                             op0=ALU.mult, op1=ALU.add))
                for (src, par, dst) in ((af, 0, e1), (bf_, 1, e2)):
                    th.append(TH(V, "tensor_tensor", [src, i16], [E], out=E3, in0=bc(src[:], [P, P, 16], 2),
                                 in1=bc(i16[:, 0:16], [P, P, 16], 1), op=ALU.is_equal))
                    th.append(TH(V, "tensor_tensor", [E, mxf], [E], out=E4, in0=E4,
                                 in1=bc(mxf[:, par:16:2, :], [P, 8, 16, 16], 2), op=ALU.mult))
                    th.append(TH(V, "tensor_reduce", [E], [dst], out=dst[:], in_=E3, axis=AX.X, op=ALU.add))
                th.append(TH(V, "scalar_tensor_tensor", [e1, e2], [eidf], out=eidf[:], in0=e1[:], scalar=128.0, in1=e2[:],
                             op0=ALU.mult, op1=ALU.add))
                th.append(TH(V, "tensor_scalar", [eidf], [eidf], out=eidf[:], in0=eidf[:], scalar1=float(NEXP - 1), scalar2=0.0,
                             op0=ALU.min, op1=ALU.max))
                th.append(TH(V, "tensor_copy", [eidf], [idx[k]], out=idx[k][:], in_=eidf[:]))
                th.append(TH(V, "tensor_tensor", [scv], [gt], out=gt[:], in0=scv[:], in1=bc(scv[:, :, 0], [P, 8, 16], 2), op=ALU.subtract))
                th.append(TH(A, "activation", [gt], [gt], out=gt[:], in_=gt[:], func=AF.Exp))
                th.append(TH(V, "tensor_reduce", [gt], [zs], out=zs[:], in_=gt[:], axis=AX.X, op=ALU.add))
                th.append(TH(V, "reciprocal", [zs], [zs], out=zs[:], in_=zs[:]))
                th.append(TH(V, "tensor_tensor", [gt, zs], [gate[k]], out=gate[k][:].rearrange("p (h k) -> p h k", h=8), in0=gt[:],
                             in1=bc(zs[:], [P, 8, 16], 2), op=ALU.mult))
                return th

            def prologue(t):
                k = t % 2
                tsl = slice(t * P, (t + 1) * P)
                D("sp", x1t[k][:], x1_d[tsl, :], [x1_d], [x1t[k]])
                D("sp", xnb[k][:], xnb_d[tsl, :], [xnb_d], [xnb[k]])

            def epilogue(t):
                k = t % 2
                tsl = slice(t * P, (t + 1) * P)
                x2 = x1t[k]
                for n in range(4):
                    nsl = slice(n * 512, (n + 1) * 512)
                    V("tensor_tensor", [accp[n], x2], [x2], out=x2[:, nsl], in0=accp[n][:], in1=x2[:, nsl], op=ALU.add)
                A("activation", [x2], [junka, ss3], out=junka[:], in_=x2[:], func=AF.Square, accum_out=ss3[:])
                rstd_from_ss(ss3, rsf3, DM)
                V("scalar_tensor_tensor", [x2, rsf3, gbl], [x2], out=x2[:], in0=x2[:], scalar=rsf3[:, 0:1], in1=gbl[:],
                  op0=ALU.mult, op1=ALU.mult)
                D("sp", out_d[tsl, :], x2[:], [x2], [out_d])

            def gather_thunk(G, j):
                t, g = divmod(G, NGT)
                k = t % 2
                s_ = g * GS + j
                slot = uv[(G * GS + j) % NR]
                GATHER(slot[:], uvb_d[:, :], idx[k][:, s_:s_ + 1], [idx[k], uvb_d], [slot])

            def dot_thunk(G, j):
                t, g = divmod(G, NGT)
                k = t % 2
                slot = uv[(G * GS + j) % NR]
                a_ = a4[G % 3]
                if j == 0:
                    V("memset", [], [a_], ap=a_[:], constant=0.0)
                V("scalar_tensor_tensor", [xnb[k]] + ([a_] if j == 0 else []), [slot] + ([a_] if j == GS - 1 else []),
                  out=slot[:, 0:DM], in0=slot[:, 0:DM], scalar=1.0, in1=xnb[k][:], op0=ALU.mult, op1=ALU.mult,
                  accum_out=a_[:, j:j + 1])

            def gelu_thunk(G, j):
                t, g = divmod(G, NGT)
                k = t % 2
                a_ = a4[G % 3]; q_ = tq[G % 2]; c_ = cf[G % 3]
                if j == 0:
                    A("activation", [a_], [q_], out=q_[:], in_=a_[:], func=AF.Gelu_apprx_tanh)
                elif j == 1:
                    V("tensor_tensor", [q_, gate[k]], [c_], out=c_[:], in0=q_[:], in1=gate[k][:, g * GS:(g + 1) * GS], op=ALU.mult)

            ndg = [0]

            def mm_thunk(G, j):
                t, g = divmod(G, NGT)
                s_ = g * GS + j
                slot = uv[(G * GS + j) % NR]
                c_ = cf[G % 3]
                d_ = dg[ndg[0] % 4]; ndg[0] += 1
                A("activation", [c_, identf], [d_], out=d_[:], in_=identf[:], func=AF.Copy, scale=c_[:, j:j + 1])
                for n in range(4):
                    M([d_, slot], [accp[n]], out=accp[n][:], lhsT=d_[:], rhs=slot[:, DM + n * 512:DM + (n + 1) * 512],
                      start=(s_ == 0), stop=(s_ == P - 1))

            D("sp", ssb[:].rearrange("p c t -> p (c t)"), sc_d[0], [sc_d], [ssb])
            for f_ in routing_thunks(0):
                f_()
            prologue(0)
            for G0 in range(2):
                for j in range(GS):
                    gather_thunk(G0, j)
            rt = []
            per_slot = 0
            for G in range(NG + 2):
                t, g = divmod(G, NGT)
                if G < NG and g == 0 and t + 1 < NT_OWN:
                    D("sp", ssb[:].rearrange("p c t -> p (c t)"), sc_d[t + 1], [sc_d], [ssb])
                    rt = routing_thunks(t + 1)
                    per_slot = -(-len(rt) // ((NGT - 6) * GS))
                if G < NG and g == NGT // 2 and t + 1 < NT_OWN:
                    prologue(t + 1)
                if G < NG and g == NGT - 4:
                    while rt:
                        rt.pop(0)()
                if 0 <= G - 1 < NG:
                    gelu_thunk(G - 1, 0)
                for j in range(GS):
                    if 0 <= G - 2 < NG:
                        mm_thunk(G - 2, j)
                    if G + 2 < NG:
                        gather_thunk(G + 2, j)
                    if G < NG:
                        dot_thunk(G, j)
                    if j == 1 and 0 <= G - 1 < NG:
                        gelu_thunk(G - 1, 1)
                    for _ in range(per_slot):
                        if rt:
                            rt.pop(0)()
                if 0 <= G - 2 < NG and (G - 2) % NGT == NGT - 1:
                    epilogue((G - 2) // NGT)
            pg.barrier()
            pg.emit()
            if stop_after == "D2":
                return nc, pg
    return nc, pg


def _pcol(v, n):
    return np.ascontiguousarray(np.asarray(v, np.float32).reshape(n, P).T)


def _blockdiag(w):
    out = np.zeros((8, P, P), np.float32)
    for c in range(8):
        out[c, 0:64, 0:64] = w[2 * c]
        out[c, 64:128, 64:128] = w[2 * c + 1]
    return out


def make_in_maps(inp):
    f32 = np.float32
    x = np.asarray(inp["x"], f32)
    L = 0
    half_d = 32
    inv = (np.float32(10000.0) ** (-np.arange(half_d, dtype=f32) / np.float32(half_d))).astype(f32)
    ident = np.eye(P, dtype=f32)
    jj = np.arange(P)[:, None]
    qq = np.arange(P)[None, :]
    mask_prev = (jj >= qq).astype(f32)
    mask_next = (jj <= qq).astype(f32)
    common = {
        "w_in": np.ascontiguousarray(inp["w_in"][L], f32),
        "w_out": np.ascontiguousarray(inp["w_out"][L], f32),
        "w_pq": np.ascontiguousarray(inp["w_pq"][L], f32),
        "u_emb": np.ascontiguousarray(inp["u_emb"][L], f32),
        "v_emb": np.ascontiguousarray(inp["v_emb"][L], f32),
        "ident": ident, "mask_prev": mask_prev, "mask_next": mask_next,
        "iota16": np.ascontiguousarray(np.broadcast_to(np.concatenate([np.arange(16, dtype=f32), np.array([16 * m for m in range(1, 16)] + [1e9], f32)])[None, :], (P, 32))),
        "iota256": np.ascontiguousarray(np.broadcast_to(np.arange(256, dtype=f32)[None, :], (P, 256))),
        "gm": _pcol(inp["g_mix"][L], 16),
        "gao": _pcol(inp["g_attn_out"][L], 8),
        "glo": _pcol(inp["g_lru_out"][L], 8),
        "sinkb": np.ascontiguousarray(np.broadcast_to(np.asarray(inp["sink"][L], f32)[None, :], (P, 16))),
        "cvb": _pcol(inp["conv_b"][L], 8),
        "gffn_b": np.ascontiguousarray(np.broadcast_to(np.asarray(inp["g_ffn"][L], f32)[None, :], (P, DM))),
        "gfinal_b": np.ascontiguousarray(np.broadcast_to(np.asarray(inp["g_final"], f32)[None, :], (P, DM))),
    }
    sk = np.stack([np.asarray(inp["sub_k1"][L], f32), np.asarray(inp["sub_k2"][L], f32)], axis=1)
    common["subkT"] = np.ascontiguousarray(sk.reshape(16, P, P).transpose(2, 0, 1))
    cw = np.asarray(inp["conv_w"][L], f32)
    dirs = {
        "fwd": [inp["fwd_wa"][L], inp["fwd_wx"][L], inp["fwd_ba"][L], inp["fwd_bx"][L], inp["fwd_lam"][L]],
        "bwd": [inp["bwd_wa"][L], inp["bwd_wx"][L], inp["bwd_ba"][L], inp["bwd_bx"][L], inp["bwd_lam"][L]],
    }
    maps = []
    for c in range(8):
        b, hf = divmod(c, 2)
        xl = x[b] if hf == 0 else x[b, ::-1]
        pos = np.arange(17 * P) if hf == 0 else (SEQ - 1 - np.arange(17 * P))
        ang = pos.astype(f32)[:, None] * inv[None, :]
        cos = np.cos(ang).astype(f32)
        sin = np.sin(ang).astype(f32)
        w5 = np.zeros((5, 1024), f32)
        if hf == 0:
            w5[0:4] = cw
        else:
            for j in range(4):
                w5[4 - j] = cw[j]
        w5p = np.stack([_pcol(w5[o], 8) for o in range(5)], axis=2).reshape(P, 40)
        d1, d2 = (dirs["fwd"], dirs["bwd"]) if hf == 0 else (dirs["bwd"], dirs["fwd"])
        gatew = np.stack([_blockdiag(np.asarray(d1[0], f32)), _blockdiag(np.asarray(d1[1], f32)),
                          _blockdiag(np.asarray(d2[0], f32)), _blockdiag(np.asarray(d2[1], f32))], axis=0)
        gateb = np.concatenate([_pcol(d1[2], 8), _pcol(d1[3], 8), _pcol(d2[2], 8), _pcol(d2[3], 8)], axis=1)
        lam = np.concatenate([_pcol(d1[4], 8), _pcol(d2[4], 8)], axis=1)
        m = dict(common)
        m.update({
            "x": np.ascontiguousarray(xl),
            "rope_cs": np.ascontiguousarray(np.concatenate([cos, cos], axis=1)),
            "rope_sn": np.ascontiguousarray(np.concatenate([-sin, sin], axis=1)),
            "w5": np.ascontiguousarray(w5p),
            "gatew": np.ascontiguousarray(gatew),
            "gateb": np.ascontiguousarray(gateb),
            "lam": np.ascontiguousarray(lam),
        })
        maps.append(m)
    return maps


_CACHE = {}


def kernel(**inputs):
    if "nc" not in _CACHE:
        _CACHE["nc"] = build_program()[0]
    nc = _CACHE["nc"]
    maps = make_in_maps(inputs)
    res = run_bass_kernel_spmd(nc, maps, core_ids=list(range(8)))
    out = np.zeros((4, SEQ, DM), np.float32)
    for c in range(8):
        b, hf = divmod(c, 2)
        o = np.asarray(res.results[c]["out"], np.float32)
        if hf == 0:
            out[b, 0:OWN] = o
        else:
            out[b, OWN:SEQ] = o[::-1]
    return out
```
